# Optimizing a Trainium2 kernel written in Bass

```python
import jax, jax.numpy as jnp
from jax import lax
import numpy as np

D_MODEL = 2048
BATCH = 32
SEQ = 256
DEPTH = 2
DEC_BATCH = 4
DEC_SEQ = 2048
PAST_LEN = 256

GRID_W = 64
N_DIR = 2
EPS = 1e-6
ROPE_BASE = 10000.0
GLA_HEADS = 6
GLA_DK = 64
GLA_DV = 128
GLA_WIDTH = GLA_HEADS * GLA_DV
GLA_QK_WIDTH = GLA_HEADS * GLA_DK
GLA_LORA = 16
GLA_CHUNK = 64
GLA_GATE_NORM = 16.0
GLA_LOG_DECAY_MIN = -1.0
CM_GROUPS = 4
CM_GROUP_DIM = 128
CM_WIDTH = CM_GROUPS * CM_GROUP_DIM
CM_CHUNK = 128
RW_HEADS = 12
RW_HEAD_DIM = 64
RW_WIDTH = RW_HEADS * RW_HEAD_DIM
RW_DECAY_LORA = 64
RW_ICLR_LORA = 64
RW_GN_EPS = 64e-5
MIX_WIDTH = GLA_WIDTH + CM_WIDTH + RW_WIDTH
SPLIT_SIZES = (GLA_QK_WIDTH, GLA_QK_WIDTH, GLA_WIDTH, GLA_WIDTH, GLA_LORA,
               CM_WIDTH, CM_WIDTH, CM_WIDTH,
               RW_WIDTH, RW_WIDTH, RW_WIDTH, RW_WIDTH, RW_DECAY_LORA, RW_ICLR_LORA)
IN_WIDTH = 7056

kernel_name = 'hybrid_gla_chunkmlp_rwkv7_dit_step'


def _rmsnorm(x, w):
    xf = x.astype(jnp.float32)
    y = xf * lax.rsqrt(jnp.mean(xf * xf, axis=-1, keepdims=True) + EPS)
    return (y * w.astype(jnp.float32)).astype(x.dtype)


def _modulation(cond, w_mod, b_mod):
    m = (jax.nn.silu(cond) @ w_mod + b_mod)[:, None, :]
    return jnp.split(m, 3, axis=-1)


def _split_cols(z):
    parts, off = [], 0
    for size in SPLIT_SIZES:
        parts.append(z[..., off:off + size])
        off += size
    return parts


def _rope_1d(x, pos):
    half = x.shape[-1] // 2
    inv = ROPE_BASE ** (-jnp.arange(half, dtype=jnp.float32) / half)
    ang = pos.astype(jnp.float32)[:, None] * inv[None, :]
    cos, sin = jnp.cos(ang)[:, None, :], jnp.sin(ang)[:, None, :]
    x1, x2 = x[..., :half], x[..., half:]
    return jnp.concatenate([x1 * cos - x2 * sin, x1 * sin + x2 * cos], axis=-1)


def _rope_grid(x):
    L = x.shape[1]
    rows = L // GRID_W
    row = jnp.repeat(jnp.arange(rows), GRID_W)
    col = jnp.tile(jnp.arange(GRID_W), rows)
    h = x.shape[-1] // 2
    return jnp.concatenate([_rope_1d(x[..., :h], row), _rope_1d(x[..., h:], col)], axis=-1)


def _gla_chunked(q, k, v, log_a, s0):
    B, L, H, dk = q.shape
    dv = v.shape[-1]
    C = GLA_CHUNK
    n = L // C
    q, k, v, log_a = [t.reshape(B, n, C, H, t.shape[-1]) for t in (q, k, v, log_a)]
    b = jnp.cumsum(log_a, axis=2)
    b_last = b[:, :, -1]
    q_t = q * jnp.exp(b)
    k_t = k * jnp.exp(-b)
    k_end = k * jnp.exp(b_last[:, :, None] - b)
    mask = jnp.tril(jnp.ones((C, C), dtype=bool))
    att = jnp.where(mask, jnp.einsum('bnihk,bnjhk->bnhij', q_t, k_t), 0.0)
    o_intra = jnp.einsum('bnhij,bnjhv->bnihv', att, v)
    kv = jnp.einsum('bnchk,bnchv->bnhkv', k_end, v)
    decay = jnp.exp(b_last)

    def step(s, inp):
        dec, kv_c = inp
        return dec[..., None] * s + kv_c, s

    s_final, s_prev = lax.scan(step, s0, (jnp.moveaxis(decay, 1, 0), jnp.moveaxis(kv, 1, 0)))
    s_prev = jnp.moveaxis(s_prev, 0, 1)
    o_inter = jnp.einsum('bnchk,bnhkv->bnchv', q_t, s_prev)
    return (o_intra + o_inter).reshape(B, L, H, dv), s_final


def _gla_branch(q, k, v, lora, lw, s0, latent):
    B, L, _ = q.shape
    f32 = jnp.float32
    q = q.astype(f32).reshape(B, L, GLA_HEADS, GLA_DK) * (GLA_DK ** -0.5)
    k = k.astype(f32).reshape(B, L, GLA_HEADS, GLA_DK)
    v = v.astype(f32).reshape(B, L, GLA_HEADS, GLA_DV)
    if latent:
        q, k = _rope_grid(q), _rope_grid(k)
    z = (jnp.einsum('blr,drk->dblk', lora.astype(f32), lw['gla_a2'].astype(f32))
         + lw['gla_a_bias'].astype(f32)[:, None, None, :])
    log_a = jnp.maximum(jax.nn.log_sigmoid(z) / GLA_GATE_NORM, GLA_LOG_DECAY_MIN)
    log_a = log_a.reshape(N_DIR, B, L, GLA_HEADS, GLA_DK)
    s0 = s0.astype(f32)
    flip = lambda t: jnp.flip(t, axis=1)
    o_f, s_f = _gla_chunked(q, k, v, log_a[0], s0[:, 0])
    o_b, s_b = _gla_chunked(flip(q), flip(k), flip(v), flip(log_a[1]), s0[:, 1])
    o = o_f + flip(o_b)
    o = (o * lax.rsqrt(jnp.mean(o * o, axis=-1, keepdims=True) + EPS)
         * lw['gla_norm_w'].astype(f32).reshape(GLA_HEADS, GLA_DV))
    return o.reshape(B, L, GLA_WIDTH), jnp.stack([s_f, s_b], axis=1)


def _chunk_mlp_branch(u, v, lw):
    B, L, _ = u.shape
    f32 = jnp.float32
    n = L // CM_CHUNK
    v = v.astype(f32).reshape(B, L, CM_GROUPS, CM_GROUP_DIM)
    v = (v * lax.rsqrt(jnp.mean(v * v, axis=-1, keepdims=True) + EPS)
         * lw['cm_norm_w'].astype(f32).reshape(CM_GROUPS, CM_GROUP_DIM))
    v = v.reshape(B, n, CM_CHUNK, CM_GROUPS, CM_GROUP_DIM)
    s = (jnp.einsum('gij,bnjgc->bnigc', lw['cm_ws'].astype(f32), v)
         + lw['cm_bs'].astype(f32).T[:, :, None])
    return u.astype(f32) * s.reshape(B, L, CM_WIDTH)


def _rwkv7_scan(r, w, k, v, a, b, s0):
    def step(s, inp):
        r_t, w_t, k_t, v_t, a_t, b_t = inp
        sa = jnp.einsum('bhvk,bhk->bhv', s, a_t)
        s = (s * w_t[:, :, None, :] + sa[..., None] * b_t[:, :, None, :]
             + v_t[..., None] * k_t[:, :, None, :])
        return s, jnp.einsum('bhvk,bhk->bhv', s, r_t)

    xs = tuple(jnp.moveaxis(t, 1, 0) for t in (r, w, k, v, a, b))
    s_final, y = lax.scan(step, s0, xs)
    return jnp.moveaxis(y, 0, 1), s_final


def _rwkv_branch(r, k, v, wl, al, lw, s0):
    B, L, _ = r.shape
    f32 = jnp.float32
    hn = (RW_HEADS, RW_HEAD_DIM)
    shp = lambda t: t.astype(f32).reshape(B, L, RW_HEADS, RW_HEAD_DIM)
    r, k, v = shp(r), shp(k), shp(v)
    w_log = -jax.nn.softplus(-(lw['rw_w0'].astype(f32)[:, None, None, :]
                               + jnp.einsum('blr,drc->dblc', jnp.tanh(wl.astype(f32)), lw['rw_w2'].astype(f32)))) - 0.5
    decay = jnp.exp(-jnp.exp(w_log)).reshape(N_DIR, B, L, RW_HEADS, RW_HEAD_DIM)
    iclr = jax.nn.sigmoid(lw['rw_a0'].astype(f32)[:, None, None, :]
                          + jnp.einsum('blr,drc->dblc', al.astype(f32), lw['rw_a2'].astype(f32)))
    iclr = iclr.reshape(N_DIR, B, L, RW_HEADS, RW_HEAD_DIM)
    kk = k * lw['rw_kk'].astype(f32).reshape(hn)
    kk = kk * lax.rsqrt(jnp.maximum(jnp.sum(kk * kk, axis=-1, keepdims=True), 1e-24))
    k_dir = k[None] * (1.0 + (iclr - 1.0) * lw['rw_ka'].astype(f32).reshape(hn))
    b_dir = kk[None] * iclr
    s0 = s0.astype(f32)
    flip = lambda t: jnp.flip(t, axis=1)
    y_f, s_f = _rwkv7_scan(r, decay[0], k_dir[0], v, -kk, b_dir[0], s0[:, 0])
    y_b, s_b = _rwkv7_scan(flip(r), flip(decay[1]), flip(k_dir[1]), flip(v), flip(-kk), flip(b_dir[1]), s0[:, 1])
    y = y_f + flip(y_b)
    mu = jnp.mean(y, axis=-1, keepdims=True)
    var = jnp.mean(jnp.square(y - mu), axis=-1, keepdims=True)
    y = (y - mu) * lax.rsqrt(var + RW_GN_EPS) * lw['rw_gn_w'].astype(f32).reshape(hn)
    bonus = jnp.sum(jnp.sum(r[None] * k_dir * lw['rw_rk'].astype(f32), axis=-1, keepdims=True), axis=0) * v
    y = y + bonus
    return y.reshape(B, L, RW_WIDTH), jnp.stack([s_f, s_b], axis=1)


def _mixer(h, lw, s0_gla, s0_rw, latent):
    z = h @ lw['w_in']
    (g_q, g_k, g_v, g_gate, g_lora, c_u, c_v, c_gate,
     r_r, r_k, r_v, r_gate, r_wl, r_al) = _split_cols(z)
    o_gla, s_gla = _gla_branch(g_q, g_k, g_v, g_lora, lw, s0_gla, latent)
    o_cm = _chunk_mlp_branch(c_u, c_v, lw)
    o_rw, s_rw = _rwkv_branch(r_r, r_k, r_v, r_wl, r_al, lw, s0_rw)
    o = jnp.concatenate([o_gla.astype(h.dtype) * jax.nn.silu(g_gate),
                         o_cm.astype(h.dtype) * jax.nn.silu(c_gate),
                         o_rw.astype(h.dtype) * jax.nn.silu(r_gate)], axis=-1)
    return o @ lw['w_out'], s_gla, s_rw


def setup_inputs(seed: int = 0) -> dict:
    key = jax.random.key(seed)
    ks = jax.random.split(key, 26)
    nrm = lambda k, shape, s: jax.random.normal(k, shape, jnp.float32) * s
    D = D_MODEL
    return {
        'x_prompt': nrm(ks[0], (BATCH, SEQ, D), 1.0),
        'x_sample': nrm(ks[1], (DEC_BATCH, DEC_SEQ, D), 1.0),
        'state_gla': nrm(ks[2], (DEC_BATCH, DEPTH, N_DIR, GLA_HEADS, GLA_DK, GLA_DV), 1.0),
        'state_rwkv': nrm(ks[3], (DEC_BATCH, DEPTH, N_DIR, RW_HEADS, RW_HEAD_DIM, RW_HEAD_DIM), 0.5),
        'c': nrm(ks[4], (DEC_BATCH, D), 1.0),
        'c_ctx': nrm(ks[5], (D,), 1.0),
        'w_mod': nrm(ks[6], (DEPTH, D, 3 * D), 0.5 * D ** -0.5),
        'b_mod': nrm(ks[7], (DEPTH, 3 * D), 0.02),
        'norm_w': 1.0 + nrm(ks[8], (DEPTH, D), 0.02),
        'w_in': nrm(ks[9], (DEPTH, D, IN_WIDTH), D ** -0.5),
        'w_out': nrm(ks[10], (DEPTH, MIX_WIDTH, D), MIX_WIDTH ** -0.5),
        'gla_a2': nrm(ks[11], (DEPTH, N_DIR, GLA_LORA, GLA_QK_WIDTH), GLA_LORA ** -0.5),
        'gla_a_bias': nrm(ks[12], (DEPTH, N_DIR, GLA_QK_WIDTH), 0.1),
        'gla_norm_w': 1.0 + nrm(ks[13], (DEPTH, GLA_WIDTH), 0.02),
        'cm_norm_w': 1.0 + nrm(ks[14], (DEPTH, CM_WIDTH), 0.02),
        'cm_ws': nrm(ks[15], (DEPTH, CM_GROUPS, CM_CHUNK, CM_CHUNK), CM_CHUNK ** -0.5),
        'cm_bs': 1.0 + nrm(ks[16], (DEPTH, CM_GROUPS, CM_CHUNK), 0.1),
        'rw_w0': nrm(ks[17], (DEPTH, N_DIR, RW_WIDTH), 0.5),
        'rw_w2': nrm(ks[18], (DEPTH, N_DIR, RW_DECAY_LORA, RW_WIDTH), 0.5 * RW_DECAY_LORA ** -0.5),
        'rw_a0': nrm(ks[19], (DEPTH, N_DIR, RW_WIDTH), 0.1),
        'rw_a2': nrm(ks[20], (DEPTH, N_DIR, RW_ICLR_LORA, RW_WIDTH), 0.5 * RW_ICLR_LORA ** -0.5),
        'rw_kk': 0.85 + nrm(ks[21], (DEPTH, RW_WIDTH), 0.02),
        'rw_ka': 1.0 + nrm(ks[22], (DEPTH, RW_WIDTH), 0.02),
        'rw_rk': nrm(ks[23], (DEPTH, RW_HEADS, RW_HEAD_DIM), 0.1),
        'rw_gn_w': 1.0 + nrm(ks[24], (DEPTH, RW_WIDTH), 0.02),
        'final_norm_w': 1.0 + nrm(ks[25], (D,), 0.02),
    }


def reference(x_prompt, x_sample, state_gla, state_rwkv, c, c_ctx, w_mod, b_mod, norm_w,
              w_in, w_out, gla_a2, gla_a_bias, gla_norm_w, cm_norm_w, cm_ws, cm_bs,
              rw_w0, rw_w2, rw_a0, rw_a2, rw_kk, rw_ka, rw_rk, rw_gn_w, final_norm_w):
    ctx, lat = x_prompt, x_sample
    b_ctx = ctx.shape[0]
    new_gla, new_rw = [], []
    for l in range(DEPTH):
        lw = {'w_in': w_in[l], 'w_out': w_out[l], 'gla_a2': gla_a2[l], 'gla_a_bias': gla_a_bias[l],
              'gla_norm_w': gla_norm_w[l], 'cm_norm_w': cm_norm_w[l], 'cm_ws': cm_ws[l], 'cm_bs': cm_bs[l],
              'rw_w0': rw_w0[l], 'rw_w2': rw_w2[l], 'rw_a0': rw_a0[l], 'rw_a2': rw_a2[l],
              'rw_kk': rw_kk[l], 'rw_ka': rw_ka[l], 'rw_rk': rw_rk[l], 'rw_gn_w': rw_gn_w[l]}
        shift, scale, gate = _modulation(c_ctx[None, :], w_mod[l], b_mod[l])
        h = _rmsnorm(ctx, norm_w[l]) * (1.0 + scale) + shift
        z_gla = jnp.zeros((b_ctx, N_DIR, GLA_HEADS, GLA_DK, GLA_DV), jnp.float32)
        z_rw = jnp.zeros((b_ctx, N_DIR, RW_HEADS, RW_HEAD_DIM, RW_HEAD_DIM), jnp.float32)
        out, s_gla, s_rw = _mixer(h, lw, z_gla, z_rw, latent=False)
        ctx = ctx + gate * out
        new_gla.append(s_gla)
        new_rw.append(s_rw)
        shift, scale, gate = _modulation(c, w_mod[l], b_mod[l])
        h = _rmsnorm(lat, norm_w[l]) * (1.0 + scale) + shift
        out, _, _ = _mixer(h, lw, state_gla[:, l], state_rwkv[:, l], latent=True)
        lat = lat + gate * out
    y_prompt = _rmsnorm(ctx, final_norm_w)
    y_sample = _rmsnorm(lat, final_norm_w)
    new_state_gla = jnp.stack(new_gla, axis=1)
    new_state_rwkv = jnp.stack(new_rw, axis=1)
    return (y_prompt, y_sample, new_state_gla, new_state_rwkv)
```

```python
import contextlib
import math
import numpy as np
import concourse.bass as bass
import concourse.mybir as mybir
from concourse.bass_utils import run_bass_kernel_spmd

F32 = mybir.dt.float32
BF16 = mybir.dt.bfloat16
AF = mybir.ActivationFunctionType
ALU = mybir.AluOpType

EPOCH = 8000
SAME_ENGINE_SYNC = True

D = 2048
T = 2048
NT = 16
KC = 16
DEPTH = 2
IN_W = 7056
EPS = 1e-6
GN_EPS = 64e-5


class Sched:
    CENG = ("pe", "act", "dve", "pool")

    def __init__(self, nc, n_dma_sems=56):
        self.nc = nc
        self.n_dma_sems = n_dma_sems
        self.streams = {e: [] for e in ("pe", "act", "dve", "pool", "sp")}
        self.nops = {e: 0 for e in self.CENG}
        self.signal = {e: set() for e in self.CENG}
        self.state = {}
        self.dma_uses = [0] * n_dma_sems
        self.dma_rng = {"pool": (0, 16), "sp": (16, 48), "act": (48, n_dma_sems)}
        self.dma_next = {q: lo for q, (lo, hi) in self.dma_rng.items()}
        self.ndma = 0
        self.final_tokens = []

    def _deps(self, reads, writes):
        deps = set()
        for k in reads:
            st = self.state.get(k)
            if st and st[0] is not None:
                deps.add(st[0])
        for k in writes:
            st = self.state.get(k)
            if st:
                if st[0] is not None:
                    deps.add(st[0])
                for t in st[1].values():
                    deps.add(t)
        return deps

    def _update(self, who, token, reads, writes):
        for k in reads:
            st = self.state.setdefault(k, [None, {}])
            st[1][who] = token
        for k in writes:
            self.state[k] = [token, {}]

    def _add_waits(self, eng, deps):
        for t in deps:
            if t[0] == "c":
                if t[1] == eng and (eng == "pe" or not SAME_ENGINE_SYNC):
                    continue
                self.signal[t[1]].add(t[2])
            self.streams[eng].append(("wait", t))

    @staticmethod
    def _excl(reads, writes):
        r2, w2 = [], []
        for src, dst_is_w in ((reads, False), (writes, True)):
            for k in src:
                kk = k[0] if isinstance(k, tuple) else k
                if isinstance(kk, str) and kk.startswith("ps") and len(kk) >= 3 and kk[2].isdigit():
                    w2.append(kk[:3])
                elif dst_is_w:
                    w2.append(k)
                else:
                    r2.append(k)
        return r2, w2

    def op(self, eng, fn, reads=(), writes=()):
        reads, writes = self._excl(reads, writes)
        deps = self._deps(reads, writes)
        self._add_waits(eng, deps)
        i = self.nops[eng]
        self.nops[eng] = i + 1
        token = ("c", eng, i)
        self.streams[eng].append(("op", fn, i))
        self._update(eng, token, reads, writes)
        return token

    def dma(self, q, fn, reads=(), writes=(), final=False):
        deps = self._deps(reads, writes)
        j = self.dma_next[q]
        lo, hi = self.dma_rng[q]
        self.dma_next[q] = lo + (j + 1 - lo) % (hi - lo)
        if self.dma_uses[j] > 0:
            deps.add(("d", j, self.dma_uses[j]))
        self._add_waits(q, deps)
        self.dma_uses[j] += 1
        token = ("d", j, self.dma_uses[j])
        self.streams[q].append(("dma", fn, j))
        self._update(("q", q, self.ndma), token, reads, writes)
        self.ndma += 1
        if final:
            self.final_tokens.append(token)
        return token

    def wait_all_outputs(self, eng="sp"):
        for t in self.final_tokens:
            self.streams[eng].append(("wait", t))

    def emit(self):
        nc = self.nc
        rank, nsig = {}, {}
        for e in self.CENG:
            srt = sorted(self.signal[e])
            rank[e] = {o: r for r, o in enumerate(srt)}
            nsig[e] = len(srt)
        with contextlib.ExitStack() as es:
            csem = {}
            for e in self.CENG:
                n_ep = max(1, (nsig[e] + EPOCH - 1) // EPOCH)
                csem[e] = [es.enter_context(nc.semaphore(f"c_{e}_{k}")) for k in range(n_ep)]
            dsem = [es.enter_context(nc.semaphore(f"d_{j}")) for j in range(self.n_dma_sems)]
            block = es.enter_context(nc.Block())

            def resolve(t):
                if t[0] == "c":
                    r = rank[t[1]][t[2]]
                    return csem[t[1]][r // EPOCH], (r % EPOCH) + 1, ("c", t[1], r // EPOCH)
                return dsem[t[1]], 16 * t[2], ("d", t[1])

            def run_stream(ename):
                items = self.streams[ename]

                def body(eng):
                    waited = {}
                    for it in items:
                        if it[0] == "wait":
                            sem, val, sk = resolve(it[1])
                            if waited.get(sk, 0) >= val:
                                continue
                            waited[sk] = val
                            eng.wait_ge(sem, val)
                        elif it[0] == "op":
                            ins = it[1](eng)
                            o = it[2]
                            if o in rank[ename]:
                                ins.then_inc(csem[ename][rank[ename][o] // EPOCH], 1)
                        else:
                            ins = it[1](eng)
                            ins.then_inc(dsem[it[2]], 16)
                return body

            block.tensor(run_stream("pe"))
            block.scalar(run_stream("act"))
            block.vector(run_stream("dve"))
            block.gpsimd(run_stream("pool"))
            block.sync(run_stream("sp"))
        return {e: (self.nops[e], nsig[e]) for e in self.CENG}


OFF = dict(q=0, k=384, v=768, g=1536, lora=2304, cu=2320, cv=2832, cg=3344,
           rr=3856, rk=4624, rv=5392, rg=6160, wl=6928, al=6992)


def _col_perm():
    cols = []
    cols += list(range(OFF["lora"], OFF["lora"] + 16))
    cols += list(range(OFF["wl"], OFF["wl"] + 64))
    cols += list(range(OFF["al"], OFF["al"] + 64))
    for p in range(3):
        cols += list(range(OFF["q"] + p * 128, OFF["q"] + (p + 1) * 128))
        cols += list(range(OFF["k"] + p * 128, OFF["k"] + (p + 1) * 128))
        cols += list(range(OFF["v"] + p * 256, OFF["v"] + (p + 1) * 256))
        cols += list(range(OFF["g"] + p * 256, OFF["g"] + (p + 1) * 256))
    for g in range(4):
        for nm in ("cu", "cv", "cg"):
            cols += list(range(OFF[nm] + g * 128, OFF[nm] + (g + 1) * 128))
    for p in range(6):
        for nm in ("rr", "rk", "rv", "rg"):
            cols += list(range(OFF[nm] + p * 128, OFF[nm] + (p + 1) * 128))
    assert len(cols) == IN_W and len(set(cols)) == IN_W
    return np.array(cols)


C_B0 = 0
C_GLA = 144
C_CM = 144 + 3 * 768
C_RW = C_CM + 4 * 384


class Build:
    def __init__(self, dbg=(), stop=None):
        self.dbg = set(dbg)
        self.stop = stop
        self.nc = nc = bass.Bass("TRN2", target_bir_lowering=False)
        self.S = Sched(nc)
        self.es = contextlib.ExitStack()
        self.dbg_outs = {}

    def din(self, name, shape, dt=F32):
        return self.nc.dram_tensor(name, list(shape), dt, kind="ExternalInput").ap()

    def dout(self, name, shape, dt=F32):
        return self.nc.dram_tensor(name, list(shape), dt, kind="ExternalOutput").ap()

    def dscr(self, name, shape, dt=F32):
        return self.nc.dram_tensor(name, list(shape), dt, kind="Internal").ap()

    def sb(self, name, shape, dt=F32):
        return self.es.enter_context(self.nc.sbuf_tensor(name, list(shape), dt))

    def mm(self, out, lhsT, rhs, start=True, stop=True, r=(), w=()):
        self.S.op("pe", lambda e: e.matmul(out, lhsT=lhsT, rhs=rhs, start=start, stop=stop), reads=r, writes=w)

    def tr(self, out, in_, ident, r=(), w=()):
        self.S.op("pe", lambda e: e.transpose(out=out, in_=in_, identity=ident), reads=r, writes=w)

    def act(self, out, in_, func, r=(), w=(), scale=None, bias=None, accum_out=None):
        kw = {}
        if scale is not None:
            kw["scale"] = scale
        if bias is not None:
            kw["bias"] = bias
        if accum_out is not None:
            kw["accum_out"] = accum_out
        self.S.op("act", lambda e: e.activation(out=out, in_=in_, func=func, **kw), reads=r, writes=w)

    def tt(self, eng, out, in0, in1, op, r=(), w=()):
        self.S.op(eng, lambda e: e.tensor_tensor(out=out, in0=in0, in1=in1, op=op), reads=r, writes=w)

    def ts(self, eng, out, in0, s1, s2, op0, op1=None, r=(), w=()):
        if op1 is None:
            self.S.op(eng, lambda e: e.tensor_scalar(out=out, in0=in0, scalar1=s1, scalar2=None, op0=op0), reads=r, writes=w)
        else:
            self.S.op(eng, lambda e: e.tensor_scalar(out=out, in0=in0, scalar1=s1, scalar2=s2, op0=op0, op1=op1), reads=r, writes=w)

    def stt(self, out, in0, scalar, in1, op0, op1, r=(), w=()):
        self.S.op("dve", lambda e: e.scalar_tensor_tensor(out=out, in0=in0, scalar=scalar, in1=in1, op0=op0, op1=op1), reads=r, writes=w)

    def cp(self, eng, out, in_, r=(), w=()):
        if eng == "act":
            self.S.op("act", lambda e: e.activation(out=out, in_=in_, func=AF.Copy), reads=r, writes=w)
        else:
            self.S.op(eng, lambda e: e.tensor_copy(out=out, in_=in_), reads=r, writes=w)

    def memset(self, eng, ap, val, w=()):
        self.S.op(eng, lambda e: e.memset(ap, val), writes=w)

    def dma(self, q, out, in_, r=(), w=(), final=False):
        self.S.dma(q, lambda e: e.dma_start(out=out, in_=in_), reads=r, writes=w, final=final)

    def dump(self, name, src_ap, shape, key, dt=F32):
        if name not in self.dbg:
            return
        o = self.dout("dbg_" + name, shape, dt)
        self.dma("sp", o, src_ap, r=[key], final=True)

    def declare(self):
        nc = self.nc
        self.x = self.din("x", [T, D])
        self.condT = self.din("condT", [128, 16])
        self.w_mod = self.din("w_mod", [DEPTH, D, 3 * D])
        self.b_mod = self.din("b_mod", [DEPTH, 24, 1, 256])
        self.nwT = self.din("nwT", [DEPTH, 128, 16])
        self.fnw_b = self.din("fnw_b", [128, D])
        self.w_in = self.din("w_in_r", [DEPTH, D, IN_W])
        self.w_out = self.din("w_out", [DEPTH, D, D])
        self.a2aug = self.din("a2aug", [DEPTH, 2, 16, 384])
        self.w2aug = self.din("w2aug", [DEPTH, 2, 64, 768])
        self.ra2aug = self.din("ra2aug", [DEPTH, 2, 64, 768])
        self.bcols = self.din("bcols", [DEPTH, 128, 2, 6, 2])
        self.gbcols = self.din("gbcols", [DEPTH, 128, 2, 3])
        self.pcols = self.din("pcols", [DEPTH, 128, 6, 3])
        self.pb = self.din("pb", [DEPTH, 128, 2048])
        self.cm_wsT = self.din("cm_wsT", [DEPTH, 4, 128, 128])
        self.cm_bsT = self.din("cm_bsT", [DEPTH, 128, 4])
        self.cosT = self.din("cosT", [128, T])
        self.sinS = self.din("sinS", [128, T])
        self.c_ident = self.din("c_ident", [128, 128])
        self.c_pm = self.din("c_pm", [128, 128])
        self.c_m4 = self.din("c_m4", [2, 128, 512])
        self.c_scanm = self.din("c_scanm", [128, 512])
        self.c_bd = self.din("c_bd", [128, 256])
        self.c_bd64 = self.din("c_bd64", [128, 128])
        self.c_bind = self.din("c_bind", [128, 2])
        self.s0_gla = self.din("s0_gla", [DEPTH, 2, 3, 128, 128])
        self.s0_rw = self.din("s0_rw", [DEPTH, 2, 6, 128, 64])
        self.carry = self.din("carry", [128, 1])
        self.y = self.dout("y", [T, D])
        self.st_gla = self.dout("st_gla", [8, DEPTH, 2, 3, 128, 128])
        self.st_rw = self.dout("st_rw", [8, DEPTH, 2, 6, 128, 64])
        self.x1 = self.dscr("x1_scr", [T, D], F32)
        self.o_hbm = self.dscr("o_scr", [T, D], BF16)

    def alloc(self):
        sb = self.sb
        self.hT = sb("hT", [128, KC, T], BF16)
        self.wb = [sb(f"wb{i}", [128, KC, 256], BF16) for i in range(2)]
        self.wb_i = 0
        self.yacc_t = sb("yacc", [128, 4096], F32)
        self.yacc = self.yacc_t[:, :].rearrange("p (a b) -> p a b", b=256)
        self.xt = [self.yacc_t[:, i * 2048:(i + 1) * 2048] for i in range(2)]
        self.dtok_t = sb("dtok", [128, 4096], BF16)
        self.dtok = self.dtok_t[:, :].rearrange("p (a b) -> p a b", b=256)
        self.xn = self.dtok_t[:, 0:2048]
        self.gate_b = sb("gate_b", [128, D], F32)
        self.pbt = sb("pbt", [128, 2048], F32)
        self.ident = sb("ident", [128, 128], BF16)
        self.pm = sb("pm", [128, 128], BF16)
        self.m4 = [sb(f"m4_{d}", [128, 512], BF16) for d in range(2)]
        self.scanm = sb("scanm", [128, 512], BF16)
        self.bd = sb("bd", [128, 256], F32)
        self.bd64 = sb("bd64", [128, 128], BF16)
        self.bd64f = sb("bd64f", [128, 128], F32)
        self.bind = sb("bind", [128, 2], BF16)
        self.csb = [[sb(f"cs{a}_{b}", [128, 512], BF16) for b in range(1)] for a in range(2)]
        self.bct = sb("bct", [128, 2, 6, 2], F32)
        self.gbt = sb("gbt", [128, 2, 3], F32)
        self.carry_t = sb("carry_t", [128, 1], F32)
        self.scT = sb("scT", [128, 16], BF16)
        self.condt = sb("condt", [128, 16], F32)
        self.ones_f = sb("ones_f", [1, 128], F32)
        self.mrow = sb("mrow", [1, 256], F32)
        self.bm = sb("bm", [1, 256], F32)
        self.modc = sb("modc", [128, 32], F32)
        self.sc1 = sb("sc1", [128, 16], F32)
        self.nwt = sb("nwt", [128, 16], F32)
        self.small = sb("small", [128, 64], F32)
        self.junk = sb("junk", [128, 256], BF16)
        self.wlal = sb("wlal", [128, T], BF16)
        self.a2t = sb("a2t", [16, 2, 384], BF16)
        self.w2ra = sb("w2ra", [128, 2, 768], BF16)
        self.pct = sb("pct", [128, 6, 3], F32)
        self.omka = sb("omka", [128, 6], F32)
        self.wsT = sb("wsT", [128, 4, 128], BF16)
        self.bsT = sb("bsT", [128, 4], F32)
        self.fA = sb("fA", [128, T], BF16)
        self.fB = sb("fB", [128, T], BF16)
        self.fC = sb("fC", [128, T], BF16)
        self.lora = self.fC[0:16, :]
        self.vtok = sb("vtok", [128, NT, 256], BF16)
        self.sg = sb("sg", [128, NT, 256], BF16)
        self.f1 = [sb(f"f1_{i}", [128, 512], F32) for i in range(4)]
        self.dAR = sb("dAR", [128, 2, T], BF16)
        self.dBK = sb("dBK", [128, 2, T], BF16)
        self.wc = sb("wc", [128, NT], F32)
        self.wcbd = [sb(f"wcbd{i}", [128, 256], F32) for i in range(1)]
        self.Hf = sb("Hf", [128, 256], F32)
        self.Hb = sb("Hb", [128, 256], BF16)
        self.t2 = sb("t2", [128, 256], F32)
        self.skt = [sb(f"skt{j}", [128, 512], BF16) for j in range(2)]
        self.sktB = [sb(f"sktB{j}", [128, 512], BF16) for j in range(2)]
        self.rfin = [[sb(f"rfin{j}_{b}", [128, 128], BF16) for b in range(2)] for j in range(2)]
        self.ot = sb("ot", [128, 256], BF16)
        self.otf = sb("otf", [128, 256], F32)
        self.wq = [[sb(f"wq{h}_{b}", [128, 384], BF16) for b in range(2)] for h in range(2)]
        self.xu = sb("xu", [128, 2, 128], BF16)
        self.bon = sb("bon", [128, NT, 2, 2], F32)
        ps = lambda name: self.es.enter_context(self.nc.psum_tensor(name, [128, 512], F32))
        self.ps = [ps(f"ps{i}") for i in range(8)]

    def xtk(self, i):
        return [("yacc", t) for t in range(8 * i, 8 * i + 8)]

    def xnk(self):
        return [("dtok", t) for t in range(8)]

    def wbuf(self):
        i = self.wb_i
        self.wb_i = (i + 1) % len(self.wb)
        return self.wb[i], f"wb{i}"

    def weight_plan(self):
        plan = []
        for l in range(DEPTH):
            for cb in range(24):
                plan.append(("mod", l, cb * 256, 256))
            plan.append(("in", l, C_B0, 144))
            for p in range(3):
                for o in (0, 256, 512):
                    plan.append(("in", l, C_GLA + p * 768 + o, 256))
            for g in range(4):
                plan.append(("in", l, C_CM + g * 384, 256))
                plan.append(("in", l, C_CM + g * 384 + 256, 128))
            for p in range(6):
                plan.append(("in", l, C_RW + p * 512, 256))
                plan.append(("in", l, C_RW + p * 512 + 256, 256))
        return plan

    def _issue_w(self, idx):
        kind, l, c0, ncols = self.wplan[idx]
        wbt, wk = self.wbuf()
        src = self.w_mod[l] if kind == "mod" else self.w_in[l]
        wv = src.rearrange("(kc p) n -> p kc n", p=128)
        self.dma("pool", wbt[:, :, 0:ncols], wv[:, :, c0:c0 + ncols], w=[wk])
        self.wissued[idx] = (wbt, wk)

    def next_w(self, kind, l, c0, ncols, prefetch=True):
        if self.stop is not None:
            wbt, wk = self.wbuf()
            src = self.w_mod[l] if kind == "mod" else self.w_in[l]
            wv = src.rearrange("(kc p) n -> p kc n", p=128)
            self.dma("pool", wbt[:, :, 0:ncols], wv[:, :, c0:c0 + ncols], w=[wk])
            return wbt, wk
        if not hasattr(self, "wplan"):
            self.wplan = self.weight_plan()
            self.widx = 0
            self.wissued = {}
        idx = self.widx
        assert self.wplan[idx] == (kind, l, c0, ncols), (self.wplan[idx], (kind, l, c0, ncols))
        if idx not in self.wissued:
            self._issue_w(idx)
        res = self.wissued.pop(idx)
        self.widx = idx + 1
        if prefetch and idx + 1 < len(self.wplan) and (idx + 1) not in self.wissued:
            self._issue_w(idx + 1)
        return res

    def load_consts(self):
        q = "pool"
        self.dma(q, self.ident[:], self.c_ident[:, :], w=["ident"])
        self.dma(q, self.pm[:], self.c_pm[:, :], w=["pm"])
        for d in range(2):
            self.dma(q, self.m4[d][:], self.c_m4[d], w=[f"m4_{d}"])
        self.dma(q, self.scanm[:], self.c_scanm[:, :], w=["scanm"])
        self.dma("sp", self.bd[:], self.c_bd[:, :], w=["bd"])
        self.dma(q, self.bd64[:], self.c_bd64[:, :], w=["bd64"])
        self.dma("sp", self.bd64f[:], self.c_bd64[:, :], w=["bd64f"])
        self.dma(q, self.bind[:], self.c_bind[:, :], w=["bind"])
        self.dma("sp", self.carry_t[:], self.carry[:, :], w=["carry"])
        self.dma("sp", self.condt[:], self.condT[:, :], w=["condt"])
        self.act(self.scT[:], self.condt[:], AF.Silu, r=["condt"], w=["scT"])
        self.memset("dve", self.ones_f[:], 1.0, w=["ones_f"])
        self.memset("dve", self.small[:, 8:9], math.log(0.125), w=["small8"])

    def load_layer_params(self, l):
        q = "pool"
        self.dma("sp", self.nwt[:], self.nwT[l], w=["nwt"])
        self.dma("sp", self.pbt[:], self.pb[l], w=["pbt"])
        self.dma(q, self.a2t[:], self.a2aug[l].rearrange("d r c -> r d c"), w=["a2t"])
        self.dma(q, self.w2ra[0:64, :, :], self.w2aug[l].rearrange("d r c -> r d c"), w=["w2ra"])
        self.dma(q, self.w2ra[64:128, :, :], self.ra2aug[l].rearrange("d r c -> r d c"), w=["w2ra"])
        self.dma("sp", self.bct[:], self.bcols[l], w=["bct"])
        self.dma("sp", self.gbt[:], self.gbcols[l], w=["gbt"])
        self.ts("dve", self.gbt[:], self.gbt[:], -1.0, None, ALU.mult, r=["gbt"], w=["gbt"])
        self.dma("sp", self.pct[:], self.pcols[l], w=["pct"])
        self.ts("dve", self.omka[:], self.pct[:, :, 1], -1.0, 1.0, ALU.mult, ALU.add, r=["pct"], w=["omka"])
        self.dma(q, self.wsT[:], self.cm_wsT[l].rearrange("g j i -> j g i"), w=["wsT"])
        self.dma("sp", self.bsT[:], self.cm_bsT[l], w=["bsT"])

    def modulation(self, l):
        wv = self.w_mod[l].rearrange("(kc p) n -> p kc n", p=128)
        psr, psc, psg = self.ps[0], self.ps[1], self.ps[2]
        for cb in range(24):
            wbt, wk = self.next_w("mod", l, cb * 256, 256)
            self.dma("sp", self.bm[:], self.b_mod[l, cb], w=["bm"])
            for kc in range(KC):
                self.mm(psr[0:1, 0:256], self.scT[:, kc:kc + 1], wbt[:, kc, :], start=(kc == 0), stop=(kc == KC - 1),
                        r=["scT", wk], w=["ps0"])
            self.tt("dve", self.mrow[:], psr[0:1, 0:256], self.bm[:], ALU.add, r=["ps0", "bm"], w=["mrow"])
            if cb < 16:
                for j in range(2):
                    col = cb * 2 + j
                    self.mm(psc[:, col:col + 1], self.mrow[0:1, j * 128:(j + 1) * 128], self.ones_f[0:1, 0:1],
                            r=["mrow", "ones_f"], w=["ps1"])
            else:
                self.mm(psg[:, 0:256], self.ones_f[0:1, :], self.mrow[0:1, :], r=["mrow", "ones_f"], w=["ps2"])
                self.cp("act", self.gate_b[:, (cb - 16) * 256:(cb - 15) * 256], psg[:, 0:256], r=["ps2"], w=["gate_b"])
            if cb == 15:
                self.cp("dve", self.modc[:], psc[:, 0:32], r=["ps1"], w=["modc"])
                self.stt(self.sc1[:], self.modc[:, 16:32], 1.0, self.nwt[:], ALU.add, ALU.mult,
                         r=["modc", "nwt"], w=["sc1"])
        self.dump(f"modc{l}", self.modc[:], [128, 32], "modc")
        self.dump(f"sc1_{l}", self.sc1[:], [128, 16], "sc1")
        self.dump(f"gate_b{l}", self.gate_b[:], [128, D], "gate_b")

    def rstd_from_ss(self, ss_ap, out_ap, n, eps, key_in, key_out):
        self.ts("dve", out_ap, ss_ap, 1.0 / n, eps, ALU.mult, ALU.add, r=[key_in], w=[key_out])
        self.act(out_ap, out_ap, AF.Sqrt, r=[key_out], w=[key_out])
        self.S.op("dve", lambda e: e.reciprocal(out=out_ap, in_=out_ap), reads=[key_out], writes=[key_out])

    def phase1(self, l):
        src = self.x if l == 0 else self.x1
        xnk = self.xnk()
        for tt_ in range(NT):
            xt, xk = self.xt[tt_ % 2], self.xtk(tt_ % 2)
            self.dma("sp", xt, src[tt_ * 128:(tt_ + 1) * 128, :], r=([("x1", tt_)] if l else []), w=xk)
            ss = self.small[:, 0:1]
            self.act(self.xn, xt, AF.Square, r=xk, w=xnk + ["small0"], accum_out=ss)
            rs = self.small[:, 1:2]
            self.rstd_from_ss(ss, rs, D, EPS, "small0", "small1")
            self.ts("dve", self.xn, xt, rs, None, ALU.mult, r=xk + ["small1"], w=xnk)
            for half in range(2):
                pst = self.ps[3 + half]
                psk = f"ps{3 + half}"
                pv = pst[:, :].bitcast(BF16)
                for j in range(8):
                    kc = half * 8 + j
                    self.tr(pv[:, j * 128:(j + 1) * 128], self.xn[:, kc * 128:(kc + 1) * 128], self.ident[:],
                            r=xnk + ["ident"], w=[psk])
                for j in range(8):
                    kc = half * 8 + j
                    dst = self.hT[:, kc, tt_ * 128:(tt_ + 1) * 128]
                    self.ts("dve", dst, pv[:, j * 128:(j + 1) * 128], self.sc1[:, kc:kc + 1],
                            self.modc[:, kc:kc + 1], ALU.mult, ALU.add, r=[psk, "sc1", "modc"], w=[("hT", tt_, kc)])
        if f"hT{l}" in self.dbg:
            o = self.dout(f"dbg_hT{l}", [128, KC, T], BF16)
            self.dma("sp", o, self.hT[:], r=[("hT", t, k) for t in range(NT) for k in range(KC)], final=True)

    def load_w(self, l, c0, ncols, prefetch=True):
        return self.next_w("in", l, c0, ncols, prefetch)

    def proj_feat(self, wbt, wk, fc0, M, tb, pst, psk):
        for kc in range(KC):
            self.mm(pst[0:M, :], wbt[:, kc, fc0:fc0 + M], self.hT[:, kc, tb * 512:(tb + 1) * 512],
                    start=(kc == 0), stop=(kc == KC - 1), r=[wk] + [("hT", tb * 4 + i, kc) for i in range(4)], w=[psk])

    def proj_tok(self, wbt, wk, c0, N, tt_, pst, psk, o0=0):
        for kc in range(KC):
            self.mm(pst[:, o0:o0 + N], self.hT[:, kc, tt_ * 128:(tt_ + 1) * 128], wbt[:, kc, c0:c0 + N],
                    start=(kc == 0), stop=(kc == KC - 1), r=[wk, ("hT", tt_, kc)], w=[psk])

    def group0(self, l):
        wbt, wk = self.load_w(l, C_B0, 144)
        for tb in range(4):
            sl = slice(tb * 512, (tb + 1) * 512)
            pst, psk = self.ps[tb % 2], f"ps{tb % 2}"
            self.proj_feat(wbt, wk, 0, 16, tb, pst, psk)
            self.cp("act", self.lora[0:16, sl], pst[0:16, :], r=[psk], w=["fC"])
            pst, psk = self.ps[2 + tb % 2], f"ps{2 + tb % 2}"
            self.proj_feat(wbt, wk, 16, 128, tb, pst, psk)
            self.act(self.wlal[0:64, sl], pst[0:64, :], AF.Tanh, r=[psk], w=["wlal"])
            self.cp("act", self.wlal[64:128, sl], pst[64:128, :], r=[psk], w=["wlal"])
        self.dump(f"lora{l}", self.lora, [16, T], "fC", BF16)
        self.dump(f"wlal{l}", self.wlal[:], [128, T], "wlal", BF16)

    def cm(self, l, g):
        wA, wAk = self.load_w(l, C_CM + g * 384, 256)
        wB, wBk = self.load_w(l, C_CM + g * 384 + 256, 128, prefetch=False)
        for tt_ in range(NT):
            pst, psk = self.ps[tt_ % 2], f"ps{tt_ % 2}"
            self.proj_tok(wA, wAk, 0, 256, tt_, pst, psk)
            self.proj_tok(wB, wBk, 0, 128, tt_, pst, psk, o0=256)
            u, v, gt = pst[:, 0:128], pst[:, 128:256], pst[:, 256:384]
            ss = self.small[:, 2:3]
            self.act(self.junk[:, 0:128], v, AF.Square, r=[psk], w=["junk", "small2"], accum_out=ss)
            rs = self.small[:, 3:4]
            self.rstd_from_ss(ss, rs, 128, EPS, "small2", "small3")
            vn = self.ot[:, 0:128]
            self.stt(vn, v, rs, self.pbt[:, 768 + g * 128:768 + (g + 1) * 128], ALU.mult, ALU.mult,
                     r=[psk, "small3", "pbt"], w=["ot"])
            sgt = self.junk[:, 128:256]
            self.act(sgt, gt, AF.Silu, r=[psk], w=["junk"])
            ug = self.otf[:, 0:128]
            self.tt("dve", ug, u, sgt, ALU.mult, r=[psk, "junk"], w=["otf"])
            ps2, ps2k = self.ps[2 + tt_ % 2], f"ps{2 + tt_ % 2}"
            self.mm(ps2[:, 0:128], self.wsT[:, g, :], vn, r=["wsT", "ot"], w=[ps2k])
            oo = self.skt[tt_ % 2][:, 0:128]
            ok = f"skt{tt_ % 2}"
            self.stt(oo, ps2[:, 0:128], self.bsT[:, g:g + 1], ug, ALU.add, ALU.mult, r=[ps2k, "bsT", "otf"], w=[ok])
            col = 768 + g * 128
            self.dma("sp", self.o_hbm[tt_ * 128:(tt_ + 1) * 128, col:col + 128], oo, r=[ok], w=[("o", tt_)])

    def seg_boundary(self, kind, l, d, p, tile_idx, pos, ncol, st_out):
        Hf, Hb = self.Hf, self.Hb
        fwd = (d == 0)
        W = 2 * ncol
        if pos == 0:
            s0 = (self.s0_gla if kind == "gla" else self.s0_rw)[l, d, p]
            self.memset("dve", Hf[:, 0:W], 0.0, w=["Hf"])
            for j in range(2):
                self.dma("sp", Hf[j * 64:(j + 1) * 64, j * ncol:(j + 1) * ncol], s0[j * 64:(j + 1) * 64, :], w=["Hf"])
            self.cp("dve", Hb[:, 0:W], Hf[:, 0:W], r=["Hf"], w=["Hb"])
        elif pos % 2 == 0:
            prev_tile = tile_idx - 1 if fwd else tile_idx + 1
            self.seg_out(kind, l, d, p, prev_tile // 2, ncol, st_out)
            self.ts("dve", Hf[:, 0:W], Hf[:, 0:W], self.carry_t[:, 0:1], None, ALU.mult,
                    r=["Hf", "carry"], w=["Hf"])
            self.cp("dve", Hb[:, 0:W], Hf[:, 0:W], r=["Hf"], w=["Hb"])

    def seg_out(self, kind, l, d, p, seg, ncol, st_out):
        for j in range(2):
            self.dma("sp", st_out[seg, l, d, p, j * 64:(j + 1) * 64, :],
                     self.Hf[j * 64:(j + 1) * 64, j * ncol:(j + 1) * ncol], r=["Hf"], final=True)

    def state_update(self, psh, pshk, W, bdt, bdk, tt_, pos):
        wcb, wcbk = self.wcbd[0], "wcbd0"
        self.ts("pool", wcb[:, 0:W], bdt[:, 0:W], self.wc[:, tt_:tt_ + 1], None, ALU.mult, r=[bdk, "wc"], w=[wcbk])
        self.tt("dve", self.t2[:, 0:W], psh[:, 0:W], self.Hf[:, 0:W], ALU.add, r=[pshk, "Hf"], w=["t2"])
        self.tt("pool", self.Hf[:, 0:W], self.t2[:, 0:W], wcb[:, 0:W], ALU.mult, r=["t2", wcbk], w=["Hf"])
        self.tt("dve", self.Hb[:, 0:W], self.t2[:, 0:W], wcb[:, 0:W], ALU.mult, r=["t2", wcbk], w=["Hb"])

    def scan_cum(self, la, lak, cum, cumk, tmp, tmpk, fwd):
        self.S.op("dve", lambda en: en.tensor_tensor_scan(out=cum[:], data0=self.scanm[:], data1=la[:],
                                                          initial=0.0, op0=ALU.mult, op1=ALU.add),
                  reads=[lak, "scanm"], writes=[cumk])
        if not fwd:
            c3 = cum[:].rearrange("p (c t) -> p c t", t=128)
            tot = c3[:, :, 127:128].to_broadcast([128, 4, 128])
            e3 = tmp[:].rearrange("p (c t) -> p c t", t=128)
            self.tt("dve", e3, tot, c3, ALU.subtract, r=[cumk], w=[tmpk])
            self.tt("dve", cum[:], tmp[:], la[:], ALU.add, r=[tmpk, lak], w=[cumk])

    def gla(self, l, p):
        wA, wAk = self.load_w(l, C_GLA + p * 768, 256)
        raw = [self.dAR[:, 0, :], self.dAR[:, 1, :]]
        for tb in range(4):
            sl = slice(tb * 512, (tb + 1) * 512)
            cst, snt = self.csb[0][0], self.csb[1][0]
            csk, snk = "cs0_0", "cs1_0"
            self.dma("pool", cst[:], self.cosT[:, sl], w=[csk])
            self.dma("pool", snt[:], self.sinS[:, sl], w=[snk])
            for which in range(2):
                pst, psk = self.ps[which], f"ps{which}"
                self.proj_feat(wA, wAk, which * 128, 128, tb, pst, psk)
                self.cp("act", raw[which][:, sl], pst[:, :], r=[psk], w=["dAR"])
                ps2, ps2k = self.ps[2 + which], f"ps{2 + which}"
                self.mm(ps2[:, :], self.pm[:], raw[which][:, sl], r=["pm", "dAR"], w=[ps2k])
                t1, t1k = self.f1[0], "f1_0"
                t2, t2k = self.f1[1], "f1_1"
                self.tt("dve", t2[:], ps2[:, :], snt[:], ALU.mult, r=[ps2k, snk], w=[t2k])
                self.tt("pool", t1[:], raw[which][:, sl], cst[:], ALU.mult, r=["dAR", csk], w=[t1k])
                dst = (self.fA if which == 0 else self.fB)
                self.tt("pool", dst[:, sl], t1[:], t2[:], ALU.add, r=[t1k, t2k], w=["fA" if which == 0 else "fB"])
        wV, wVk = self.load_w(l, C_GLA + p * 768 + 256, 256)
        for tt_ in range(NT):
            pst, psk = self.ps[4 + tt_ % 2], f"ps{4 + tt_ % 2}"
            self.proj_tok(wV, wVk, 0, 256, tt_, pst, psk)
            self.cp("dve", self.vtok[:, tt_, :], pst[:, 0:256], r=[psk], w=[("vtok", tt_)])
        wG, wGk = self.load_w(l, C_GLA + p * 768 + 512, 256)
        for tt_ in range(NT):
            pst, psk = self.ps[6 + tt_ % 2], f"ps{6 + tt_ % 2}"
            self.proj_tok(wG, wGk, 0, 256, tt_, pst, psk)
            self.act(self.sg[:, tt_, :], pst[:, 0:256], AF.Silu, r=[psk], w=[("sg", tt_)])
        self.dump(f"gla_q{l}_{p}", self.fA[:], [128, T], "fA", BF16)
        self.dump(f"gla_k{l}_{p}", self.fB[:], [128, T], "fB", BF16)
        for d in range(2):
            fwd = (d == 0)
            for tb in range(4):
                sl = slice(tb * 512, (tb + 1) * 512)
                pst, psk = self.ps[tb % 2], f"ps{tb % 2}"
                self.mm(pst[:, :], self.a2t[:, d, p * 128:(p + 1) * 128], self.lora[:, sl], r=["a2t", "fC"], w=[psk])
                e, ek = self.f1[0], "f1_0"
                la, lak = self.f1[1], "f1_1"
                cum, cumk = self.f1[2], "f1_2"
                self.act(e[:], pst[:, :], AF.Exp, r=[psk, "gbt"], w=[ek], scale=-1.0, bias=self.gbt[:, d, p:p + 1])
                self.act(e[:], e[:], AF.Ln, r=[ek], w=[ek], bias=1.0)
                self.ts("dve", la[:], e[:], -1.0 / 16.0, -1.0, ALU.mult, ALU.max, r=[ek], w=[lak])
                self.scan_cum(la, lak, cum, cumk, e, ek, fwd)
                c3 = cum[:].rearrange("p (c t) -> p c t", t=128)
                endcol = c3[:, :, 127] if fwd else c3[:, :, 0]
                self.act(self.wc[:, tb * 4:(tb + 1) * 4], endcol, AF.Exp, r=[cumk], w=["wc"])
                e1, e1k = self.f1[0], "f1_0"
                e2, e2k = self.f1[3], "f1_3"
                self.act(e1[:], cum[:], AF.Exp, r=[cumk, "small8"], w=[e1k], bias=self.small[:, 8:9])
                self.act(e2[:], cum[:], AF.Exp, r=[cumk], w=[e2k], scale=-1.0)
                self.tt("dve", self.dBK[:, 0, sl], self.fA[:, sl], e1[:], ALU.mult, r=["fA", e1k], w=["dBK"])
                self.tt("pool", self.dBK[:, 1, sl], self.fB[:, sl], e2[:], ALU.mult, r=["fB", e2k], w=["dBK"])
                pst2, ps2k = self.ps[2 + tb % 2], f"ps{2 + tb % 2}"
                pv = pst2[:, :].bitcast(BF16)
                for i in range(4):
                    tt_ = tb * 4 + i
                    self.tr(pv[:, i * 128:(i + 1) * 128], self.dBK[:, 1, tt_ * 128:(tt_ + 1) * 128], self.ident[:],
                            r=["dBK", "ident"], w=[ps2k])
                for i in range(4):
                    tt_ = tb * 4 + i
                    self.cp("dve", self.dtok[:, tt_, 0:128], pv[:, i * 128:(i + 1) * 128], r=[ps2k], w=[("dtok", tt_)])
            self.dump(f"gla_qt{l}_{p}_{d}", self.dBK[:, 0, :], [128, T], "dBK", BF16)
            self.dump(f"gla_kt{l}_{p}_{d}", self.dBK[:, 1, :], [128, T], "dBK", BF16)
            order = list(range(NT)) if fwd else list(range(NT - 1, -1, -1))
            for pos, tt_ in enumerate(order):
                self.seg_boundary("gla", l, d, p, tt_, pos, 128, self.st_gla)
                tsl = slice(tt_ * 128, (tt_ + 1) * 128)
                def gla_scores(pos_, t_):
                    sk_, skk_ = self.skt[pos_ % 2], f"skt{pos_ % 2}"
                    ts_ = slice(t_ * 128, (t_ + 1) * 128)
                    for j in range(2):
                        rows = slice(j * 64, (j + 1) * 64)
                        pss, pssk = self.ps[j], f"ps{j}"
                        self.mm(pss[:, 0:128], self.dBK[rows, 1, ts_], self.dBK[rows, 0, ts_], r=["dBK"], w=[pssk])
                        self.tt("dve", sk_[:, j * 128:(j + 1) * 128], pss[:, 0:128], self.m4[d][:, 128:256], ALU.mult,
                                r=[pssk, f"m4_{d}"], w=[skk_])
                if pos == 0:
                    gla_scores(0, tt_)
                if pos + 1 < NT:
                    gla_scores(pos + 1, order[pos + 1])
                skt, sktk = self.skt[pos % 2], f"skt{pos % 2}"
                psy, psyk = self.ps[5 + pos % 2], f"ps{5 + pos % 2}"
                self.mm(psy[:, 0:256], self.dBK[:, 0, tsl], self.Hb[:, 0:256], start=True, stop=False,
                        r=["dBK", "Hb"], w=[psyk])
                for j in range(2):
                    self.mm(psy[:, j * 128:(j + 1) * 128], skt[:, j * 128:(j + 1) * 128],
                            self.vtok[:, tt_, j * 128:(j + 1) * 128], start=False, stop=(j == 1),
                            r=[sktk, ("vtok", tt_)], w=[psyk])
                if d == 0:
                    self.cp("act", self.yacc[:, tt_, :], psy[:, 0:256], r=[psyk], w=[("yacc", tt_)])
                else:
                    self.tt("dve", self.yacc[:, tt_, :], psy[:, 0:256], self.yacc[:, tt_, :], ALU.add,
                            r=[psyk, ("yacc", tt_)], w=[("yacc", tt_)])
                psh, pshk = self.ps[7], "ps7"
                self.mm(psh[:, 0:256], self.dtok[:, tt_, 0:128], self.vtok[:, tt_, :], r=[("dtok", tt_), ("vtok", tt_)],
                        w=[pshk])
                self.state_update(psh, pshk, 256, self.bd, "bd", tt_, pos)
            self.seg_out("gla", l, d, p, order[-1] // 2, 128, self.st_gla)
        self.dump(f"gla_y{l}_{p}", self.yacc_t[:, :], [128, 4096], ("yacc", 0))
        for tt_ in range(NT):
            for j in range(2):
                self.act(self.junk[:, 0:128], self.yacc[:, tt_, j * 128:(j + 1) * 128], AF.Square,
                         r=[("yacc", tt_)], w=["junk", "small10"], accum_out=self.small[:, 10 + j:11 + j])
            self.rstd_from_ss(self.small[:, 10:12], self.small[:, 12:14], 128, EPS, "small10", "small12")
            for j in range(2):
                self.stt(self.otf[:, j * 128:(j + 1) * 128], self.yacc[:, tt_, j * 128:(j + 1) * 128],
                         self.small[:, 12 + j:13 + j], self.pbt[:, p * 256 + j * 128:p * 256 + (j + 1) * 128],
                         ALU.mult, ALU.mult, r=[("yacc", tt_), "small12", "pbt"], w=["otf"])
            self.tt("pool", self.ot[:], self.otf[:], self.sg[:, tt_, :], ALU.mult, r=["otf", ("sg", tt_)], w=["ot"])
            col = p * 256
            self.dma("sp", self.o_hbm[tt_ * 128:(tt_ + 1) * 128, col:col + 256], self.ot[:], r=["ot"], w=[("o", tt_)])

    def rwkv(self, l, p):
        import os
        CUT = float(os.environ.get("RW_CUT", "99"))
        C0 = math.exp(-0.5)
        wA, wAk = self.load_w(l, C_RW + p * 512, 256)
        dsts = [(self.fA, "fA"), (self.fB, "fB")]
        for which in range(2):
            for tb in range(4):
                sl = slice(tb * 512, (tb + 1) * 512)
                pst, psk = self.ps[tb % 2], f"ps{tb % 2}"
                self.proj_feat(wA, wAk, which * 128, 128, tb, pst, psk)
                self.cp("act", dsts[which][0][:, sl], pst[:, :], r=[psk], w=[dsts[which][1]])
        if CUT <= 0.2:
            return
        wB, wBk = self.load_w(l, C_RW + p * 512 + 256, 256)
        kkw = self.pct[:, p, 0:1]
        ka = self.pct[:, p, 1:2]
        rk = self.pct[:, p, 2:3]
        omka = self.omka[:, p:p + 1]
        for tb in range(4):
            sl = slice(tb * 512, (tb + 1) * 512)
            sq = self.dAR[:, 0, sl]
            self.act(sq, self.fB[:, sl], AF.Square, r=["fB", "pct"], w=["dAR"], scale=kkw)
            ps2, ps2k = self.ps[2 + tb % 2], f"ps{2 + tb % 2}"
            self.mm(ps2[:, :], self.bd64[:], sq, r=["bd64", "dAR"], w=[ps2k])
            rn, rnk = self.f1[0], "f1_0"
            self.ts("dve", rn[:], ps2[:, :], 1e-24, None, ALU.max, r=[ps2k], w=[rnk])
            self.act(rn[:], rn[:], AF.Sqrt, r=[rnk], w=[rnk])
            self.S.op("dve", lambda e, rn=rn: e.reciprocal(out=rn[:], in_=rn[:]), reads=[rnk], writes=[rnk])
            self.stt(self.fC[:, sl], self.fB[:, sl], kkw, rn[:], ALU.mult, ALU.mult, r=["fB", "pct", rnk], w=["fC"])
        if CUT <= 0.5:
            return
        for tt_ in range(NT):
            pst, psk = self.ps[4 + tt_ % 2], f"ps{4 + tt_ % 2}"
            self.proj_tok(wB, wBk, 0, 128, tt_, pst, psk)
            self.cp("dve", self.vtok[:, tt_, 0:128], pst[:, 0:128], r=[psk], w=[("vtok", tt_)])
        for tt_ in range(NT):
            pst, psk = self.ps[6 + tt_ % 2], f"ps{6 + tt_ % 2}"
            self.proj_tok(wB, wBk, 128, 128, tt_, pst, psk)
            self.act(self.sg[:, tt_, 0:128], pst[:, 0:128], AF.Silu, r=[psk], w=[("sg", tt_)])
        if CUT <= 1:
            return
        self.dump(f"rw_r{l}_{p}", self.fA[:], [128, T], "fA", BF16)
        self.dump(f"rw_kk{l}_{p}", self.fC[:], [128, T], "fC", BF16)
        for d in range(2):
            fwd = (d == 0)
            for tb in range(4):
                sl = slice(tb * 512, (tb + 1) * 512)
                pst, psk = self.ps[tb % 2], f"ps{tb % 2}"
                self.mm(pst[:, :], self.w2ra[0:64, d, p * 128:(p + 1) * 128], self.wlal[0:64, sl], r=["w2ra", "wlal"], w=[psk])
                la, lak = self.f1[1], "f1_1"
                cum, cumk = self.f1[2], "f1_2"
                tmp, tmpk = self.f1[0], "f1_0"
                ex, exk = self.f1[3], "f1_3"
                self.act(la[:], pst[:, :], AF.Sigmoid, r=[psk, "bct"], w=[lak], bias=self.bct[:, d, p, 0:1])
                self.scan_cum(la, lak, cum, cumk, tmp, tmpk, fwd)
                c3 = cum[:].rearrange("p (c t) -> p c t", t=128)
                endcol = c3[:, :, 127] if fwd else c3[:, :, 0]
                self.act(self.wc[:, tb * 4:(tb + 1) * 4], endcol, AF.Exp, r=[cumk], w=["wc"], scale=-C0)
                self.act(ex[:], cum[:], AF.Exp, r=[cumk], w=[exk], scale=-C0)
                self.tt("dve", self.dAR[:, 1, sl], self.fA[:, sl], ex[:], ALU.mult, r=["fA", exk], w=["dAR"])
                self.tt("pool", tmp[:], cum[:], la[:], ALU.subtract, r=[cumk, lak], w=[tmpk])
                self.act(tmp[:], tmp[:], AF.Exp, r=[tmpk], w=[tmpk], scale=-C0)
                self.stt(self.dAR[:, 0, sl], self.fC[:, sl], -1.0, tmp[:], ALU.mult, ALU.mult, r=["fC", tmpk], w=["dAR"])
                self.act(ex[:], cum[:], AF.Exp, r=[cumk], w=[exk], scale=C0)
                pst2, ps2k = self.ps[2 + tb % 2], f"ps{2 + tb % 2}"
                self.mm(pst2[:, :], self.w2ra[64:128, d, p * 128:(p + 1) * 128], self.wlal[64:128, sl], r=["w2ra", "wlal"], w=[ps2k])
                ic, ick = self.f1[1], "f1_1"
                self.act(ic[:], pst2[:, :], AF.Sigmoid, r=[ps2k, "bct"], w=[ick], bias=self.bct[:, d, p, 1:2])
                self.tt("pool", tmp[:], ic[:], ex[:], ALU.mult, r=[ick, exk], w=[tmpk])
                self.tt("dve", self.dBK[:, 0, sl], self.fC[:, sl], tmp[:], ALU.mult, r=["fC", tmpk], w=["dBK"])
                self.ts("dve", ic[:], ic[:], ka, omka, ALU.mult, ALU.add, r=[ick, "pct", "omka"], w=[ick])
                self.tt("pool", ic[:], ic[:], self.fB[:, sl], ALU.mult, r=[ick, "fB"], w=[ick])
                self.tt("dve", self.dBK[:, 1, sl], ic[:], ex[:], ALU.mult, r=[ick, exk], w=["dBK"])
                bprod = self.f1[0][:, :].bitcast(BF16)[:, 0:512]
                self.stt(bprod, ic[:], rk, self.fA[:, sl], ALU.mult, ALU.mult, r=[ick, "pct", "fA"], w=[tmpk])
                psb, psbk = self.ps[4 + tb % 2], f"ps{4 + tb % 2}"
                for i in range(4):
                    self.mm(psb[:, 2 * i:2 * i + 2], bprod[:, i * 128:(i + 1) * 128], self.bind[:], r=[tmpk, "bind"], w=[psbk])
                self.cp("dve", self.bon[:, tb * 4:(tb + 1) * 4, d, :],
                        psb[:, 0:8].rearrange("p (a b) -> p a b", b=2), r=[psbk], w=["bon"])
                pst3, ps3k = self.ps[6 + tb % 2], f"ps{6 + tb % 2}"
                pv = pst3[:, :].bitcast(BF16)
                for i in range(4):
                    tt_ = tb * 4 + i
                    for which in range(2):
                        self.tr(pv[:, i * 256 + which * 128:i * 256 + (which + 1) * 128],
                                self.dBK[:, which, tt_ * 128:(tt_ + 1) * 128], self.ident[:], r=["dBK", "ident"], w=[ps3k])
                for i in range(4):
                    tt_ = tb * 4 + i
                    self.cp("dve", self.dtok[:, tt_, :], pv[:, i * 256:(i + 1) * 256], r=[ps3k], w=[("dtok", tt_)])
            if CUT <= 2:
                return
            self.dump(f"rw_ar{l}_{p}_{d}", self.dAR[:], [128, 2, T], "dAR", BF16)
            self.dump(f"rw_bk{l}_{p}_{d}", self.dBK[:], [128, 2, T], "dBK", BF16)
            order = list(range(NT)) if fwd else list(range(NT - 1, -1, -1))
            si_steps = self.rw_SI(d, order[0], 0)
            for st_ in si_steps:
                st_()
            for pos, tt_ in enumerate(order):
                par = pos % 2
                c_steps = self.rw_C(l, d, p, tt_, pos, par)
                n_steps = self.rw_SI(d, order[pos + 1], 1 - par) if pos + 1 < NT else []
                for i in range(max(len(c_steps), len(n_steps))):
                    if i < len(n_steps):
                        n_steps[i]()
                    if i < len(c_steps):
                        c_steps[i]()
            self.seg_out("rw", l, d, p, order[-1] // 2, 64, self.st_rw)
            if CUT <= 5:
                return
        if CUT <= 6:
            return
        self.dump(f"rw_y{l}_{p}", self.yacc_t[:, :], [128, 4096], ("yacc", 0))
        for tt_ in range(NT):
            yv = self.yacc[:, tt_, 0:128]
            st = self.small[:, 16:28].rearrange("p (a b) -> p a b", b=6)
            for j in range(2):
                self.S.op("dve", lambda e, j=j, yv=yv, st=st: e.bn_stats(out=st[:, j, :], in_=yv[:, j * 64:(j + 1) * 64]),
                          reads=[("yacc", tt_)], writes=["small16"])
            mv = self.small[:, 28:32].rearrange("p (a b) -> p a b", b=2)
            for j in range(2):
                self.S.op("dve", lambda e, j=j, st=st, mv=mv: e.bn_aggr(out=mv[:, j, :], in_=st[:, j, :]),
                          reads=["small16"], writes=["small28"])
            rs = self.small[:, 32:34]
            self.ts("dve", rs, mv[:, :, 1], GN_EPS, None, ALU.add, r=["small28"], w=["small32"])
            self.act(rs, rs, AF.Sqrt, r=["small32"], w=["small32"])
            self.S.op("dve", lambda e, rs=rs: e.reciprocal(out=rs, in_=rs), reads=["small32"], writes=["small32"])
            bsum = self.small[:, 34:36]
            self.tt("dve", bsum, self.bon[:, tt_, 0, :], self.bon[:, tt_, 1, :], ALU.add, r=["bon"], w=["small34"])
            for j in range(2):
                cs = slice(j * 64, (j + 1) * 64)
                self.ts("dve", self.otf[:, cs], yv[:, cs], mv[:, j, 0:1], self.small[:, 32 + j:33 + j], ALU.subtract, ALU.mult,
                        r=[("yacc", tt_), "small28", "small32"], w=["otf"])
            self.tt("pool", self.otf[:, 0:128], self.otf[:, 0:128], self.pbt[:, 1280 + p * 128:1280 + (p + 1) * 128], ALU.mult,
                    r=["otf", "pbt"], w=["otf"])
            for j in range(2):
                cs = slice(j * 64, (j + 1) * 64)
                self.stt(self.otf[:, cs], self.vtok[:, tt_, cs], self.small[:, 34 + j:35 + j], self.otf[:, cs], ALU.mult, ALU.add,
                         r=[("vtok", tt_), "small34", "otf"], w=["otf"])
            self.tt("pool", self.ot[:, 0:128], self.otf[:, 0:128], self.sg[:, tt_, 0:128], ALU.mult, r=["otf", ("sg", tt_)], w=["ot"])
            col = 1280 + p * 128
            self.dma("sp", self.o_hbm[tt_ * 128:(tt_ + 1) * 128, col:col + 128], self.ot[:, 0:128], r=["ot"], w=[("o", tt_)])

    def rw_SI(self, d, tt_, par):
        tsl = slice(tt_ * 128, (tt_ + 1) * 128)
        sk = self.skt if par == 0 else self.sktB
        skk = [f"skt{j}" if par == 0 else f"sktB{j}" for j in range(2)]
        bank = [self.ps[3], self.ps[4]]
        bkk = ["ps3", "ps4"]
        steps = []

        def s_scores():
            for j in range(2):
                rows = slice(j * 64, (j + 1) * 64)
                pss, pssk = self.ps[j], f"ps{j}"
                self.mm(pss[:, 0:256], self.dBK[rows, 0, tsl], self.dAR[rows, :, tsl], r=["dBK", "dAR"], w=[pssk])
                self.mm(pss[:, 256:512], self.dBK[rows, 1, tsl], self.dAR[rows, :, tsl], r=["dBK", "dAR"], w=[pssk])
                self.tt("dve", sk[j][:, :], pss[:, :], self.m4[d][:, :], ALU.mult, r=[pssk, f"m4_{d}"], w=[skk[j]])
            pv = self.ps[2][:, :].bitcast(BF16)
            for j in range(2):
                self.tr(pv[:, j * 128:(j + 1) * 128], sk[j][:, 0:128], self.ident[:], r=[skk[j], "ident"], w=["ps2"])
            for j in range(2):
                self.cp("dve", self.wq[j][0][:, 0:128], pv[:, j * 128:(j + 1) * 128], r=["ps2"], w=[("wqpq", j, 0)])
        steps.append(s_scores)

        def s_lvl0():
            for j in range(2):
                Q0 = sk[j][:, 0:128]
                P0 = self.wq[j][0][:, 0:128]
                self.mm(bank[j][:, 0:128], Q0, P0, r=[("wqpq", j, 0), skk[j]], w=[bkk[j]])
                self.mm(bank[j][:, 128:256], P0, Q0, r=[("wqpq", j, 0), skk[j]], w=[bkk[j]])
            for j in range(2):
                Q0 = sk[j][:, 0:128]
                self.cp("act", self.wq[j][1][:, 0:256], bank[j][:, 0:256], r=[bkk[j]], w=[("wqpq", j, 1)])
                self.tt("pool", self.wq[j][1][:, 256:384], Q0, self.ident[:], ALU.add, r=[skk[j], "ident"], w=[("wqr", j, 1)])
        steps.append(s_lvl0)

        def mk_lvl(k):
            def s_lvl():
                b = k % 2
                for j in range(2):
                    W = self.wq[j][b]
                    Pk, Qk, Rk = W[:, 0:128], W[:, 128:256], W[:, 256:384]
                    if k < 6:
                        self.mm(bank[j][:, 0:128], Qk, Pk, r=[("wqpq", j, b)], w=[bkk[j]])
                        self.mm(bank[j][:, 128:384], Pk, W[:, 128:384], r=[("wqpq", j, b), ("wqr", j, b)], w=[bkk[j]])
                    else:
                        self.mm(bank[j][:, 256:384], Pk, Rk, r=[("wqpq", j, b), ("wqr", j, b)], w=[bkk[j]])
                for j in range(2):
                    Rk = self.wq[j][b][:, 256:384]
                    if k < 6:
                        self.cp("act", self.wq[j][1 - b][:, 0:256], bank[j][:, 0:256], r=[bkk[j]], w=[("wqpq", j, 1 - b)])
                        self.tt("dve", self.wq[j][1 - b][:, 256:384], bank[j][:, 256:384], Rk, ALU.add,
                                r=[bkk[j], ("wqr", j, b)], w=[("wqr", j, 1 - b)])
                    else:
                        self.tt("dve", self.rfin[j][par][:], bank[j][:, 256:384], Rk, ALU.add,
                                r=[bkk[j], ("wqr", j, b)], w=[("rfin", j, par)])
            return s_lvl
        for k in range(1, 7):
            steps.append(mk_lvl(k))
        return steps

    def rw_C(self, l, d, p, tt_, pos, par):
        tsl = slice(tt_ * 128, (tt_ + 1) * 128)
        sk = self.skt if par == 0 else self.sktB
        skk = [f"skt{j}" if par == 0 else f"sktB{j}" for j in range(2)]
        psx, psxk = self.ps[5], "ps5"
        psy, psyk = self.ps[6], "ps6"
        psh, pshk = self.ps[7], "ps7"

        def c_x():
            self.seg_boundary("rw", l, d, p, tt_, pos, 64, self.st_rw)
            self.mm(psx[:, 0:128], self.dAR[:, 0, tsl], self.Hb[:, 0:128], start=True, stop=False, r=["dAR", "Hb"], w=[psxk])
            for j in range(2):
                self.mm(psx[:, j * 64:(j + 1) * 64], sk[j][:, 256:384], self.vtok[:, tt_, j * 64:(j + 1) * 64],
                        start=False, stop=(j == 1), r=[skk[j], ("vtok", tt_)], w=[psxk])
            self.cp("act", self.xu[:, 0, :], psx[:, 0:128], r=[psxk], w=["xu0"])

        def c_u():
            for j in range(2):
                self.mm(psx[:, 128 + j * 64:128 + (j + 1) * 64], self.rfin[j][par][:], self.xu[:, 0, j * 64:(j + 1) * 64],
                        r=[("rfin", j, par), "xu0"], w=[psxk])
            self.cp("act", self.xu[:, 1, :], psx[:, 128:256], r=[psxk], w=["xu1"])

        def c_h():
            self.mm(psh[:, 0:128], self.dtok[:, tt_, 0:128], self.xu[:, 1, :], start=True, stop=False,
                    r=[("dtok", tt_), "xu1"], w=[pshk])
            self.mm(psh[:, 0:128], self.dtok[:, tt_, 128:256], self.vtok[:, tt_, 0:128], start=False, stop=True,
                    r=[("dtok", tt_), ("vtok", tt_)], w=[pshk])

        def c_y():
            self.mm(psy[:, 0:128], self.dAR[:, 1, tsl], self.Hb[:, 0:128], start=True, stop=False, r=["dAR", "Hb"], w=[psyk])
            for j in range(2):
                cs = slice(j * 64, (j + 1) * 64)
                self.mm(psy[:, cs], sk[j][:, 128:256], self.xu[:, 1, cs], start=False, stop=False,
                        r=[skk[j], "xu1"], w=[psyk])
                self.mm(psy[:, cs], sk[j][:, 384:512], self.vtok[:, tt_, cs], start=False, stop=(j == 1),
                        r=[skk[j], ("vtok", tt_)], w=[psyk])
            self.state_update(psh, pshk, 128, self.bd64f, "bd64f", tt_, pos)
            if d == 0:
                self.cp("act", self.yacc[:, tt_, 0:128], psy[:, 0:128], r=[psyk], w=[("yacc", tt_)])
            else:
                self.tt("dve", self.yacc[:, tt_, 0:128], psy[:, 0:128], self.yacc[:, tt_, 0:128], ALU.add,
                        r=[psyk, ("yacc", tt_)], w=[("yacc", tt_)])
        return [c_x, c_u, c_h, c_y]

    def rwkv_all(self, l):
        for p in range(6):
            self.rwkv(l, p)

    def phase3(self, l, last):
        wv = self.w_out[l].rearrange("(kc p) n -> p kc n", p=128)
        for nb in range(4):
            self.dma("pool", self.hT[:, :, nb * 512:(nb + 1) * 512], wv[:, :, nb * 512:(nb + 1) * 512],
                     w=[("hT", nb * 4 + i, k) for i in range(4) for k in range(KC)])
        if last:
            self.dma("sp", self.pbt[:], self.fnw_b[:, :], w=["pbt"])
        dst = self.y if last else self.x1
        src = self.x if l == 0 else self.x1
        obf = self.vtok[:, 0:8, :].rearrange("p a b -> p (a b)")
        obk = [("vtok", i) for i in range(8)]
        oT = self.vtok[:, 8:16, :].rearrange("p a b -> p (a b)")
        oTk = [("vtok", 8 + i) for i in range(8)]
        for tt_ in range(NT):
            rows = slice(tt_ * 128, (tt_ + 1) * 128)
            self.dma("sp", obf, self.o_hbm[rows, :], r=[("o", tt_)], w=obk)
            for half in range(2):
                pst, psk = self.ps[half], f"ps{half}"
                pv = pst[:, :].bitcast(BF16)
                for j in range(8):
                    kc = half * 8 + j
                    self.tr(pv[:, j * 128:(j + 1) * 128], obf[:, kc * 128:(kc + 1) * 128], self.ident[:],
                            r=obk + ["ident"], w=[psk])
                self.cp("dve", oT[:, half * 1024:(half + 1) * 1024], pv[:, :], r=[psk], w=oTk[half * 4:half * 4 + 4])
            xt, xk = self.xt[tt_ % 2], self.xtk(tt_ % 2)
            self.dma("sp", xt, src[rows, :], r=([("x1", tt_)] if l else []), w=xk)
            for nb in range(4):
                pst, psk = self.ps[2 + nb], f"ps{2 + nb}"
                for kc in range(KC):
                    self.mm(pst[:, :], oT[:, kc * 128:(kc + 1) * 128], self.hT[:, kc, nb * 512:(nb + 1) * 512],
                            start=(kc == 0), stop=(kc == KC - 1),
                            r=oTk + [("hT", nb * 4 + i, kc) for i in range(4)], w=[psk])
                sl = slice(nb * 512, (nb + 1) * 512)
                t1, t1k = self.f1[nb % 2], f"f1_{nb % 2}"
                self.tt("dve", t1[:], pst[:, :], self.gate_b[:, sl], ALU.mult, r=[psk, "gate_b"], w=[t1k])
                self.tt("pool", xt[:, sl], xt[:, sl], t1[:], ALU.add, r=xk + [t1k], w=xk)
            if last:
                ss = self.small[:, 4:5]
                jk = self.dAR[:, :, :].rearrange("p a b -> p (a b)")[:, 0:2048]
                self.act(jk, xt, AF.Square, r=xk, w=["dAR", "small4"], accum_out=ss)
                rs = self.small[:, 5:6]
                self.rstd_from_ss(ss, rs, D, EPS, "small4", "small5")
                self.stt(xt, xt, rs, self.pbt[:], ALU.mult, ALU.mult, r=xk + ["small5", "pbt"], w=xk)
                self.dma("sp", dst[rows, :], xt, r=xk, final=True)
            else:
                self.dma("sp", dst[rows, :], xt, r=xk, w=[("x1", tt_)])

    def build(self):
        self.declare()
        self.alloc()
        self.load_consts()
        for l in range(DEPTH):
            self.load_layer_params(l)
            self.modulation(l)
            if self.stop == f"mod_{l}":
                break
            self.phase1(l)
            if self.stop == f"p1_{l}":
                break
            self.group0(l)
            if self.stop == f"g0_{l}":
                break
            if self.stop == f"p3only_{l}":
                self.phase3(l, True)
                break
            if self.stop == f"rwonly_{l}":
                self.rwkv(l, 0)
                break
            if self.stop == f"cmonly_{l}":
                self.cm(l, 1)
                break
            ng = 1 if self.stop == f"gla1_{l}" else 3
            for p in range(ng):
                self.gla(l, p)
            if self.stop in (f"gla_{l}", f"gla1_{l}"):
                break
            for g in range(4):
                self.cm(l, g)
            if self.stop == f"cm_{l}":
                break
            nr = 1 if self.stop == f"rw1_{l}" else 6
            for p in range(nr):
                self.rwkv(l, p)
            if self.stop in (f"rw_{l}", f"rw1_{l}"):
                break
            self.phase3(l, l == DEPTH - 1)
        self.S.wait_all_outputs("sp")
        stats = self.S.emit()
        self.es.close()
        return stats


def _rope_tables(latent):
    cosT = np.ones((128, T), np.float32)
    sinS = np.zeros((128, T), np.float32)
    if latent:
        t = np.arange(T)
        row, col = t // 64, t % 64
        inv = (10000.0 ** (-np.arange(16, dtype=np.float32) / 16)).astype(np.float32)
        for dd in range(128):
            d64 = dd % 64
            blk = d64 // 32
            i = d64 % 16
            second = (d64 % 32) >= 16
            pos = (row if blk == 0 else col).astype(np.float32)
            ang = pos * inv[i]
            cosT[dd] = np.cos(ang)
            sinS[dd] = np.sin(ang) if second else -np.sin(ang)
    return cosT, sinS


def _consts():
    c = {}
    c["c_ident"] = np.eye(128, dtype=np.float32)
    pm = np.zeros((128, 128), np.float32)
    for dout_ in range(128):
        d32 = dout_ % 32
        partner = dout_ + 16 if d32 < 16 else dout_ - 16
        pm[partner, dout_] = 1.0
    c["c_pm"] = pm
    s = np.arange(128)[:, None]
    t = np.arange(128)[None, :]
    m4 = np.zeros((2, 128, 512), np.float32)
    for d, (strict, incl) in enumerate([((s < t), (s <= t)), ((s > t), (s >= t))]):
        m4[d, :, 0:128] = strict
        m4[d, :, 128:256] = incl
        m4[d, :, 256:384] = strict
        m4[d, :, 384:512] = incl
    c["c_m4"] = m4
    sm = np.ones((128, 512), np.float32)
    sm[:, ::128] = 0.0
    c["c_scanm"] = sm
    bd = np.zeros((128, 256), np.float32)
    bd[0:64, 0:128] = 1.0
    bd[64:128, 128:256] = 1.0
    c["c_bd"] = bd
    bd64 = np.zeros((128, 128), np.float32)
    bd64[0:64, 0:64] = 1.0
    bd64[64:128, 64:128] = 1.0
    c["c_bd64"] = bd64
    bind = np.zeros((128, 2), np.float32)
    bind[0:64, 0] = 1.0
    bind[64:128, 1] = 1.0
    c["c_bind"] = bind
    return c


def prep_inputs(inp):
    f = lambda a: np.ascontiguousarray(np.asarray(a, dtype=np.float32))
    perm = _col_perm()
    shared = {}
    shared["w_mod"] = f(inp["w_mod"])
    shared["b_mod"] = f(inp["b_mod"]).reshape(DEPTH, 24, 1, 256)
    shared["nwT"] = f(f(inp["norm_w"]).reshape(DEPTH, 16, 128).transpose(0, 2, 1))
    shared["fnw_b"] = f(np.broadcast_to(f(inp["final_norm_w"])[None, :], (128, D)))
    shared["w_in_r"] = f(f(inp["w_in"])[:, :, perm])
    shared["w_out"] = f(inp["w_out"])
    shared["a2aug"] = f(inp["gla_a2"])
    shared["w2aug"] = f(inp["rw_w2"])
    shared["ra2aug"] = f(inp["rw_a2"])
    bc = np.stack([f(inp["rw_w0"]).reshape(DEPTH, 2, 6, 128), f(inp["rw_a0"]).reshape(DEPTH, 2, 6, 128)], axis=-1)
    shared["bcols"] = f(bc.transpose(0, 3, 1, 2, 4))
    shared["gbcols"] = f(f(inp["gla_a_bias"]).reshape(DEPTH, 2, 3, 128).transpose(0, 3, 1, 2))
    pc = np.stack([f(inp["rw_kk"]).reshape(DEPTH, 6, 128), f(inp["rw_ka"]).reshape(DEPTH, 6, 128),
                   f(inp["rw_rk"]).reshape(DEPTH, 6, 128)], axis=-1)
    shared["pcols"] = f(pc.transpose(0, 2, 1, 3))
    pbrow = np.concatenate([f(inp["gla_norm_w"]), f(inp["cm_norm_w"]), f(inp["rw_gn_w"])], axis=1)
    shared["pb"] = f(np.broadcast_to(pbrow[:, None, :], (DEPTH, 128, 2048)))
    shared["cm_wsT"] = f(f(inp["cm_ws"]).transpose(0, 1, 3, 2))
    shared["cm_bsT"] = f(f(inp["cm_bs"]).transpose(0, 2, 1))
    shared.update(_consts())
    cos_l, sin_l = _rope_tables(True)
    cos_c, sin_c = _rope_tables(False)
    xp = f(inp["x_prompt"])
    xs = f(inp["x_sample"])
    sg = f(inp["state_gla"])
    sr = f(inp["state_rwkv"])
    maps = []
    for c in range(8):
        m = dict(shared)
        if c < 4:
            m["x"] = f(xp[8 * c:8 * c + 8].reshape(T, D))
            m["condT"] = f(f(inp["c_ctx"]).reshape(16, 128).T)
            m["cosT"], m["sinS"] = cos_c, sin_c
            m["s0_gla"] = np.zeros((DEPTH, 2, 3, 128, 128), np.float32)
            m["s0_rw"] = np.zeros((DEPTH, 2, 6, 128, 64), np.float32)
            m["carry"] = np.zeros((128, 1), np.float32)
        else:
            b = c - 4
            m["x"] = f(xs[b])
            m["condT"] = f(f(inp["c"])[b].reshape(16, 128).T)
            m["cosT"], m["sinS"] = cos_l, sin_l
            m["s0_gla"] = f(sg[b].reshape(DEPTH, 2, 3, 128, 128))
            m["s0_rw"] = f(sr[b].transpose(0, 1, 2, 4, 3).reshape(DEPTH, 2, 6, 128, 64))
            m["carry"] = np.ones((128, 1), np.float32)
        maps.append(m)
    return maps


_CACHE = {}


def kernel(**inputs):
    maps = prep_inputs(inputs)
    if "nc" not in _CACHE:
        b = Build()
        b.build()
        _CACHE["nc"] = b.nc
    res = run_bass_kernel_spmd(_CACHE["nc"], maps, core_ids=list(range(8)))
    R = res.results
    y_prompt = np.stack([R[c]["y"] for c in range(4)]).reshape(32, 256, D).astype(np.float32)
    y_sample = np.stack([R[c]["y"] for c in range(4, 8)]).astype(np.float32)
    sgl = np.stack([R[c]["st_gla"] for c in range(4)]).reshape(32, DEPTH, 2, 6, 64, 128).astype(np.float32)
    srw = np.stack([R[c]["st_rw"] for c in range(4)]).reshape(32, DEPTH, 2, 12, 64, 64)
    srw = np.ascontiguousarray(srw.transpose(0, 1, 2, 3, 5, 4)).astype(np.float32)
    return (y_prompt, y_sample, sgl, srw)
```

```python
import contextlib
import math
import numpy as np
import concourse.bass as bass
import concourse.mybir as mybir
from concourse.bass_utils import run_bass_kernel_spmd

F32 = mybir.dt.float32
BF16 = mybir.dt.bfloat16
AF = mybir.ActivationFunctionType
ALU = mybir.AluOpType

EPOCH = 8000
SAME_ENGINE_SYNC = True

D = 2048
T = 2048
NT = 16
KC = 16
DEPTH = 2
IN_W = 7056
EPS = 1e-6
GN_EPS = 64e-5


class Sched:
    CENG = ("pe", "act", "dve", "pool")

    def __init__(self, nc, n_dma_sems=56):
        self.nc = nc
        self.n_dma_sems = n_dma_sems
        self.streams = {e: [] for e in ("pe", "act", "dve", "pool", "sp")}
        self.nops = {e: 0 for e in self.CENG}
        self.signal = {e: set() for e in self.CENG}
        self.state = {}
        self.dma_uses = [0] * n_dma_sems
        self.dma_rng = {"pool": (0, 16), "sp": (16, 48), "act": (48, n_dma_sems)}
        self.dma_next = {q: lo for q, (lo, hi) in self.dma_rng.items()}
        self.ndma = 0
        self.final_tokens = []

    def _deps(self, reads, writes):
        deps = set()
        for k in reads:
            st = self.state.get(k)
            if st and st[0] is not None:
                deps.add(st[0])
        for k in writes:
            st = self.state.get(k)
            if st:
                if st[0] is not None:
                    deps.add(st[0])
                for t in st[1].values():
                    deps.add(t)
        return deps

    def _update(self, who, token, reads, writes):
        for k in reads:
            st = self.state.setdefault(k, [None, {}])
            st[1][who] = token
        for k in writes:
            self.state[k] = [token, {}]

    def _add_waits(self, eng, deps):
        for t in deps:
            if t[0] == "c":
                if t[1] == eng and (eng == "pe" or not SAME_ENGINE_SYNC):
                    continue
                self.signal[t[1]].add(t[2])
            self.streams[eng].append(("wait", t))

    @staticmethod
    def _excl(reads, writes):
        r2, w2 = [], []
        for src, dst_is_w in ((reads, False), (writes, True)):
            for k in src:
                kk = k[0] if isinstance(k, tuple) else k
                if isinstance(kk, str) and kk.startswith("ps") and len(kk) >= 3 and kk[2].isdigit():
                    w2.append(kk[:3])
                elif dst_is_w:
                    w2.append(k)
                else:
                    r2.append(k)
        return r2, w2

    def op(self, eng, fn, reads=(), writes=()):
        reads, writes = self._excl(reads, writes)
        deps = self._deps(reads, writes)
        self._add_waits(eng, deps)
        i = self.nops[eng]
        self.nops[eng] = i + 1
        token = ("c", eng, i)
        self.streams[eng].append(("op", fn, i))
        self._update(eng, token, reads, writes)
        return token

    def dma(self, q, fn, reads=(), writes=(), final=False):
        deps = self._deps(reads, writes)
        j = self.dma_next[q]
        lo, hi = self.dma_rng[q]
        self.dma_next[q] = lo + (j + 1 - lo) % (hi - lo)
        if self.dma_uses[j] > 0:
            deps.add(("d", j, self.dma_uses[j]))
        self._add_waits(q, deps)
        self.dma_uses[j] += 1
        token = ("d", j, self.dma_uses[j])
        self.streams[q].append(("dma", fn, j))
        self._update(("q", q, self.ndma), token, reads, writes)
        self.ndma += 1
        if final:
            self.final_tokens.append(token)
        return token

    def wait_all_outputs(self, eng="sp"):
        for t in self.final_tokens:
            self.streams[eng].append(("wait", t))

    def emit(self):
        nc = self.nc
        rank, nsig = {}, {}
        for e in self.CENG:
            srt = sorted(self.signal[e])
            rank[e] = {o: r for r, o in enumerate(srt)}
            nsig[e] = len(srt)
        with contextlib.ExitStack() as es:
            csem = {}
            for e in self.CENG:
                n_ep = max(1, (nsig[e] + EPOCH - 1) // EPOCH)
                csem[e] = [es.enter_context(nc.semaphore(f"c_{e}_{k}")) for k in range(n_ep)]
            dsem = [es.enter_context(nc.semaphore(f"d_{j}")) for j in range(self.n_dma_sems)]
            block = es.enter_context(nc.Block())

            def resolve(t):
                if t[0] == "c":
                    r = rank[t[1]][t[2]]
                    return csem[t[1]][r // EPOCH], (r % EPOCH) + 1, ("c", t[1], r // EPOCH)
                return dsem[t[1]], 16 * t[2], ("d", t[1])

            def run_stream(ename):
                items = self.streams[ename]

                def body(eng):
                    waited = {}
                    for it in items:
                        if it[0] == "wait":
                            sem, val, sk = resolve(it[1])
                            if waited.get(sk, 0) >= val:
                                continue
                            waited[sk] = val
                            eng.wait_ge(sem, val)
                        elif it[0] == "op":
                            ins = it[1](eng)
                            o = it[2]
                            if o in rank[ename]:
                                ins.then_inc(csem[ename][rank[ename][o] // EPOCH], 1)
                        else:
                            ins = it[1](eng)
                            ins.then_inc(dsem[it[2]], 16)
                return body

            block.tensor(run_stream("pe"))
            block.scalar(run_stream("act"))
            block.vector(run_stream("dve"))
            block.gpsimd(run_stream("pool"))
            block.sync(run_stream("sp"))
        return {e: (self.nops[e], nsig[e]) for e in self.CENG}


OFF = dict(q=0, k=384, v=768, g=1536, lora=2304, cu=2320, cv=2832, cg=3344,
           rr=3856, rk=4624, rv=5392, rg=6160, wl=6928, al=6992)


def _col_perm():
    cols = []
    cols += list(range(OFF["lora"], OFF["lora"] + 16))
    cols += list(range(OFF["wl"], OFF["wl"] + 64))
    cols += list(range(OFF["al"], OFF["al"] + 64))
    for p in range(3):
        cols += list(range(OFF["q"] + p * 128, OFF["q"] + (p + 1) * 128))
        cols += list(range(OFF["k"] + p * 128, OFF["k"] + (p + 1) * 128))
        cols += list(range(OFF["v"] + p * 256, OFF["v"] + (p + 1) * 256))
        cols += list(range(OFF["g"] + p * 256, OFF["g"] + (p + 1) * 256))
    for g in range(4):
        for nm in ("cu", "cv", "cg"):
            cols += list(range(OFF[nm] + g * 128, OFF[nm] + (g + 1) * 128))
    for p in range(6):
        for nm in ("rr", "rk", "rv", "rg"):
            cols += list(range(OFF[nm] + p * 128, OFF[nm] + (p + 1) * 128))
    assert len(cols) == IN_W and len(set(cols)) == IN_W
    return np.array(cols)


C_B0 = 0
C_GLA = 144
C_CM = 144 + 3 * 768
C_RW = C_CM + 4 * 384


class Build:
    def __init__(self, dbg=(), stop=None):
        self.dbg = set(dbg)
        self.stop = stop
        self.nc = nc = bass.Bass("TRN2", target_bir_lowering=False)
        self.S = Sched(nc)
        self.es = contextlib.ExitStack()
        self.dbg_outs = {}

    def din(self, name, shape, dt=F32):
        return self.nc.dram_tensor(name, list(shape), dt, kind="ExternalInput").ap()

    def dout(self, name, shape, dt=F32):
        return self.nc.dram_tensor(name, list(shape), dt, kind="ExternalOutput").ap()

    def dscr(self, name, shape, dt=F32):
        return self.nc.dram_tensor(name, list(shape), dt, kind="Internal").ap()

    def sb(self, name, shape, dt=F32):
        return self.es.enter_context(self.nc.sbuf_tensor(name, list(shape), dt))

    def mm(self, out, lhsT, rhs, start=True, stop=True, r=(), w=()):
        self.S.op("pe", lambda e: e.matmul(out, lhsT=lhsT, rhs=rhs, start=start, stop=stop), reads=r, writes=w)

    def tr(self, out, in_, ident, r=(), w=()):
        self.S.op("pe", lambda e: e.transpose(out=out, in_=in_, identity=ident), reads=r, writes=w)

    def act(self, out, in_, func, r=(), w=(), scale=None, bias=None, accum_out=None):
        kw = {}
        if scale is not None:
            kw["scale"] = scale
        if bias is not None:
            kw["bias"] = bias
        if accum_out is not None:
            kw["accum_out"] = accum_out
        self.S.op("act", lambda e: e.activation(out=out, in_=in_, func=func, **kw), reads=r, writes=w)

    def tt(self, eng, out, in0, in1, op, r=(), w=()):
        self.S.op(eng, lambda e: e.tensor_tensor(out=out, in0=in0, in1=in1, op=op), reads=r, writes=w)

    def ts(self, eng, out, in0, s1, s2, op0, op1=None, r=(), w=()):
        if op1 is None:
            self.S.op(eng, lambda e: e.tensor_scalar(out=out, in0=in0, scalar1=s1, scalar2=None, op0=op0), reads=r, writes=w)
        else:
            self.S.op(eng, lambda e: e.tensor_scalar(out=out, in0=in0, scalar1=s1, scalar2=s2, op0=op0, op1=op1), reads=r, writes=w)

    def stt(self, out, in0, scalar, in1, op0, op1, r=(), w=()):
        self.S.op("dve", lambda e: e.scalar_tensor_tensor(out=out, in0=in0, scalar=scalar, in1=in1, op0=op0, op1=op1), reads=r, writes=w)

    def cp(self, eng, out, in_, r=(), w=()):
        if eng == "act":
            self.S.op("act", lambda e: e.activation(out=out, in_=in_, func=AF.Copy), reads=r, writes=w)
        else:
            self.S.op(eng, lambda e: e.tensor_copy(out=out, in_=in_), reads=r, writes=w)

    def memset(self, eng, ap, val, w=()):
        self.S.op(eng, lambda e: e.memset(ap, val), writes=w)

    def dma(self, q, out, in_, r=(), w=(), final=False):
        self.S.dma(q, lambda e: e.dma_start(out=out, in_=in_), reads=r, writes=w, final=final)

    def dump(self, name, src_ap, shape, key, dt=F32):
        if name not in self.dbg:
            return
        o = self.dout("dbg_" + name, shape, dt)
        self.dma("sp", o, src_ap, r=[key], final=True)

    def declare(self):
        nc = self.nc
        self.x = self.din("x", [T, D])
        self.condT = self.din("condT", [128, 16])
        self.w_mod = self.din("w_mod", [DEPTH, D, 3 * D])
        self.b_mod = self.din("b_mod", [DEPTH, 24, 1, 256])
        self.nwT = self.din("nwT", [DEPTH, 128, 16])
        self.fnw_b = self.din("fnw_b", [128, D])
        self.w_in = self.din("w_in_r", [DEPTH, D, IN_W])
        self.w_out = self.din("w_out", [DEPTH, D, D])
        self.a2aug = self.din("a2aug", [DEPTH, 2, 16, 384])
        self.w2aug = self.din("w2aug", [DEPTH, 2, 64, 768])
        self.ra2aug = self.din("ra2aug", [DEPTH, 2, 64, 768])
        self.bcols = self.din("bcols", [DEPTH, 128, 2, 6, 2])
        self.gbcols = self.din("gbcols", [DEPTH, 128, 2, 3])
        self.pcols = self.din("pcols", [DEPTH, 128, 6, 3])
        self.pb = self.din("pb", [DEPTH, 128, 2048])
        self.cm_wsT = self.din("cm_wsT", [DEPTH, 4, 128, 128])
        self.cm_bsT = self.din("cm_bsT", [DEPTH, 128, 4])
        self.cosT = self.din("cosT", [128, T])
        self.sinS = self.din("sinS", [128, T])
        self.c_ident = self.din("c_ident", [128, 128])
        self.c_pm = self.din("c_pm", [128, 128])
        self.c_m4 = self.din("c_m4", [2, 128, 512])
        self.c_scanm = self.din("c_scanm", [128, 512])
        self.c_bd = self.din("c_bd", [128, 256])
        self.c_bd64 = self.din("c_bd64", [128, 128])
        self.c_bind = self.din("c_bind", [128, 2])
        self.s0_gla = self.din("s0_gla", [DEPTH, 2, 3, 128, 128])
        self.s0_rw = self.din("s0_rw", [DEPTH, 2, 6, 128, 64])
        self.carry = self.din("carry", [128, 1])
        self.y = self.dout("y", [T, D])
        self.st_gla = self.dout("st_gla", [8, DEPTH, 2, 3, 128, 128])
        self.st_rw = self.dout("st_rw", [8, DEPTH, 2, 6, 128, 64])
        self.x1 = self.dscr("x1_scr", [T, D], F32)
        self.o_hbm = self.dscr("o_scr", [T, D], BF16)

    def alloc(self):
        sb = self.sb
        self.hT = sb("hT", [128, KC, T], BF16)
        self.wb = [sb(f"wb{i}", [128, KC, 256], BF16) for i in range(2)]
        self.wb_i = 0
        self.yacc_t = sb("yacc", [128, 4096], F32)
        self.yacc = self.yacc_t[:, :].rearrange("p (a b) -> p a b", b=256)
        self.xt = [self.yacc_t[:, i * 2048:(i + 1) * 2048] for i in range(2)]
        self.dtok_t = sb("dtok", [128, 4096], BF16)
        self.dtok = self.dtok_t[:, :].rearrange("p (a b) -> p a b", b=256)
        self.xn = self.dtok_t[:, 0:2048]
        self.gate_b = sb("gate_b", [128, D], F32)
        self.pbt = sb("pbt", [128, 2048], F32)
        self.ident = sb("ident", [128, 128], BF16)
        self.pm = sb("pm", [128, 128], BF16)
        self.m4 = [sb(f"m4_{d}", [128, 512], BF16) for d in range(2)]
        self.scanm = sb("scanm", [128, 512], BF16)
        self.bd = sb("bd", [128, 256], F32)
        self.bd64 = sb("bd64", [128, 128], BF16)
        self.bd64f = sb("bd64f", [128, 128], F32)
        self.bind = sb("bind", [128, 2], BF16)
        self.csb = [[sb(f"cs{a}_{b}", [128, 512], BF16) for b in range(1)] for a in range(2)]
        self.bct = sb("bct", [128, 2, 6, 2], F32)
        self.gbt = sb("gbt", [128, 2, 3], F32)
        self.carry_t = sb("carry_t", [128, 1], F32)
        self.scT = sb("scT", [128, 16], BF16)
        self.condt = sb("condt", [128, 16], F32)
        self.ones_f = sb("ones_f", [1, 128], F32)
        self.mrow = sb("mrow", [1, 256], F32)
        self.bm = sb("bm", [1, 256], F32)
        self.modc = sb("modc", [128, 32], F32)
        self.sc1 = sb("sc1", [128, 16], F32)
        self.nwt = sb("nwt", [128, 16], F32)
        self.small = sb("small", [128, 64], F32)
        self.junk = sb("junk", [128, 256], BF16)
        self.wlal = sb("wlal", [128, T], BF16)
        self.a2t = sb("a2t", [16, 2, 384], BF16)
        self.w2ra = sb("w2ra", [128, 2, 768], BF16)
        self.pct = sb("pct", [128, 6, 3], F32)
        self.omka = sb("omka", [128, 6], F32)
        self.wsT = sb("wsT", [128, 4, 128], BF16)
        self.bsT = sb("bsT", [128, 4], F32)
        self.fA = sb("fA", [128, T], BF16)
        self.fB = sb("fB", [128, T], BF16)
        self.fC = sb("fC", [128, T], BF16)
        self.lora = self.fC[0:16, :]
        self.vtok = sb("vtok", [128, NT, 256], BF16)
        self.sg = sb("sg", [128, NT, 256], BF16)
        self.f1 = [sb(f"f1_{i}", [128, 512], F32) for i in range(4)]
        self.dAR = sb("dAR", [128, 2, T], BF16)
        self.dBK = sb("dBK", [128, 2, T], BF16)
        self.wc = sb("wc", [128, NT], F32)
        self.wcbd = [sb(f"wcbd{i}", [128, 256], F32) for i in range(1)]
        self.Hf = sb("Hf", [128, 256], F32)
        self.Hb = sb("Hb", [128, 256], BF16)
        self.t2 = sb("t2", [128, 256], F32)
        self.skt = [sb(f"skt{j}", [128, 512], BF16) for j in range(2)]
        self.sktB = [sb(f"sktB{j}", [128, 512], BF16) for j in range(2)]
        self.rfin = [[sb(f"rfin{j}_{b}", [128, 128], BF16) for b in range(2)] for j in range(2)]
        self.ot = sb("ot", [128, 256], BF16)
        self.otf = sb("otf", [128, 256], F32)
        self.pq = [[sb(f"pq{h}_{b}", [128, 256], BF16) for b in range(2)] for h in range(2)]
        self.rr = [[sb(f"rr{h}_{b}", [128, 128], BF16) for b in range(2)] for h in range(2)]
        self.xu = sb("xu", [128, 2, 128], BF16)
        self.bon = sb("bon", [128, NT, 2, 2], F32)
        ps = lambda name: self.es.enter_context(self.nc.psum_tensor(name, [128, 512], F32))
        self.ps = [ps(f"ps{i}") for i in range(8)]

    def xtk(self, i):
        return [("yacc", t) for t in range(8 * i, 8 * i + 8)]

    def xnk(self):
        return [("dtok", t) for t in range(8)]

    def wbuf(self):
        i = self.wb_i
        self.wb_i = (i + 1) % len(self.wb)
        return self.wb[i], f"wb{i}"

    def weight_plan(self):
        plan = []
        for l in range(DEPTH):
            for cb in range(24):
                plan.append(("mod", l, cb * 256, 256))
            plan.append(("in", l, C_B0, 144))
            for p in range(3):
                for o in (0, 256, 512):
                    plan.append(("in", l, C_GLA + p * 768 + o, 256))
            for g in range(4):
                plan.append(("in", l, C_CM + g * 384, 256))
                plan.append(("in", l, C_CM + g * 384 + 256, 128))
            for p in range(6):
                plan.append(("in", l, C_RW + p * 512, 256))
                plan.append(("in", l, C_RW + p * 512 + 256, 256))
        return plan

    def _issue_w(self, idx):
        kind, l, c0, ncols = self.wplan[idx]
        wbt, wk = self.wbuf()
        src = self.w_mod[l] if kind == "mod" else self.w_in[l]
        wv = src.rearrange("(kc p) n -> p kc n", p=128)
        self.dma("pool", wbt[:, :, 0:ncols], wv[:, :, c0:c0 + ncols], w=[wk])
        self.wissued[idx] = (wbt, wk)

    def next_w(self, kind, l, c0, ncols, prefetch=True):
        if self.stop is not None:
            wbt, wk = self.wbuf()
            src = self.w_mod[l] if kind == "mod" else self.w_in[l]
            wv = src.rearrange("(kc p) n -> p kc n", p=128)
            self.dma("pool", wbt[:, :, 0:ncols], wv[:, :, c0:c0 + ncols], w=[wk])
            return wbt, wk
        if not hasattr(self, "wplan"):
            self.wplan = self.weight_plan()
            self.widx = 0
            self.wissued = {}
        idx = self.widx
        assert self.wplan[idx] == (kind, l, c0, ncols), (self.wplan[idx], (kind, l, c0, ncols))
        if idx not in self.wissued:
            self._issue_w(idx)
        res = self.wissued.pop(idx)
        self.widx = idx + 1
        if prefetch and idx + 1 < len(self.wplan) and (idx + 1) not in self.wissued:
            self._issue_w(idx + 1)
        return res

    def load_consts(self):
        q = "pool"
        self.dma(q, self.ident[:], self.c_ident[:, :], w=["ident"])
        self.dma(q, self.pm[:], self.c_pm[:, :], w=["pm"])
        for d in range(2):
            self.dma(q, self.m4[d][:], self.c_m4[d], w=[f"m4_{d}"])
        self.dma(q, self.scanm[:], self.c_scanm[:, :], w=["scanm"])
        self.dma("sp", self.bd[:], self.c_bd[:, :], w=["bd"])
        self.dma(q, self.bd64[:], self.c_bd64[:, :], w=["bd64"])
        self.dma("sp", self.bd64f[:], self.c_bd64[:, :], w=["bd64f"])
        self.dma(q, self.bind[:], self.c_bind[:, :], w=["bind"])
        self.dma("sp", self.carry_t[:], self.carry[:, :], w=["carry"])
        self.dma("sp", self.condt[:], self.condT[:, :], w=["condt"])
        self.act(self.scT[:], self.condt[:], AF.Silu, r=["condt"], w=["scT"])
        self.memset("dve", self.ones_f[:], 1.0, w=["ones_f"])
        self.memset("dve", self.small[:, 8:9], math.log(0.125), w=["small8"])

    def load_layer_params(self, l):
        q = "pool"
        self.dma("sp", self.nwt[:], self.nwT[l], w=["nwt"])
        self.dma("sp", self.pbt[:], self.pb[l], w=["pbt"])
        self.dma(q, self.a2t[:], self.a2aug[l].rearrange("d r c -> r d c"), w=["a2t"])
        self.dma(q, self.w2ra[0:64, :, :], self.w2aug[l].rearrange("d r c -> r d c"), w=["w2ra"])
        self.dma(q, self.w2ra[64:128, :, :], self.ra2aug[l].rearrange("d r c -> r d c"), w=["w2ra"])
        self.dma("sp", self.bct[:], self.bcols[l], w=["bct"])
        self.dma("sp", self.gbt[:], self.gbcols[l], w=["gbt"])
        self.ts("dve", self.gbt[:], self.gbt[:], -1.0, None, ALU.mult, r=["gbt"], w=["gbt"])
        self.dma("sp", self.pct[:], self.pcols[l], w=["pct"])
        self.ts("dve", self.omka[:], self.pct[:, :, 1], -1.0, 1.0, ALU.mult, ALU.add, r=["pct"], w=["omka"])
        self.dma(q, self.wsT[:], self.cm_wsT[l].rearrange("g j i -> j g i"), w=["wsT"])
        self.dma("sp", self.bsT[:], self.cm_bsT[l], w=["bsT"])

    def modulation(self, l):
        wv = self.w_mod[l].rearrange("(kc p) n -> p kc n", p=128)
        psr, psc, psg = self.ps[0], self.ps[1], self.ps[2]
        for cb in range(24):
            wbt, wk = self.next_w("mod", l, cb * 256, 256)
            self.dma("sp", self.bm[:], self.b_mod[l, cb], w=["bm"])
            for kc in range(KC):
                self.mm(psr[0:1, 0:256], self.scT[:, kc:kc + 1], wbt[:, kc, :], start=(kc == 0), stop=(kc == KC - 1),
                        r=["scT", wk], w=["ps0"])
            self.tt("dve", self.mrow[:], psr[0:1, 0:256], self.bm[:], ALU.add, r=["ps0", "bm"], w=["mrow"])
            if cb < 16:
                for j in range(2):
                    col = cb * 2 + j
                    self.mm(psc[:, col:col + 1], self.mrow[0:1, j * 128:(j + 1) * 128], self.ones_f[0:1, 0:1],
                            r=["mrow", "ones_f"], w=["ps1"])
            else:
                self.mm(psg[:, 0:256], self.ones_f[0:1, :], self.mrow[0:1, :], r=["mrow", "ones_f"], w=["ps2"])
                self.cp("act", self.gate_b[:, (cb - 16) * 256:(cb - 15) * 256], psg[:, 0:256], r=["ps2"], w=["gate_b"])
            if cb == 15:
                self.cp("dve", self.modc[:], psc[:, 0:32], r=["ps1"], w=["modc"])
                self.stt(self.sc1[:], self.modc[:, 16:32], 1.0, self.nwt[:], ALU.add, ALU.mult,
                         r=["modc", "nwt"], w=["sc1"])
        self.dump(f"modc{l}", self.modc[:], [128, 32], "modc")
        self.dump(f"sc1_{l}", self.sc1[:], [128, 16], "sc1")
        self.dump(f"gate_b{l}", self.gate_b[:], [128, D], "gate_b")

    def rstd_from_ss(self, ss_ap, out_ap, n, eps, key_in, key_out):
        self.ts("dve", out_ap, ss_ap, 1.0 / n, eps, ALU.mult, ALU.add, r=[key_in], w=[key_out])
        self.act(out_ap, out_ap, AF.Sqrt, r=[key_out], w=[key_out])
        self.S.op("dve", lambda e: e.reciprocal(out=out_ap, in_=out_ap), reads=[key_out], writes=[key_out])

    def phase1(self, l):
        src = self.x if l == 0 else self.x1
        xnk = self.xnk()
        for tt_ in range(NT):
            xt, xk = self.xt[tt_ % 2], self.xtk(tt_ % 2)
            self.dma("sp", xt, src[tt_ * 128:(tt_ + 1) * 128, :], r=([("x1", tt_)] if l else []), w=xk)
            ss = self.small[:, 0:1]
            self.act(self.xn, xt, AF.Square, r=xk, w=xnk + ["small0"], accum_out=ss)
            rs = self.small[:, 1:2]
            self.rstd_from_ss(ss, rs, D, EPS, "small0", "small1")
            self.ts("dve", self.xn, xt, rs, None, ALU.mult, r=xk + ["small1"], w=xnk)
            for half in range(2):
                pst = self.ps[3 + half]
                psk = f"ps{3 + half}"
                pv = pst[:, :].bitcast(BF16)
                for j in range(8):
                    kc = half * 8 + j
                    self.tr(pv[:, j * 128:(j + 1) * 128], self.xn[:, kc * 128:(kc + 1) * 128], self.ident[:],
                            r=xnk + ["ident"], w=[psk])
                for j in range(8):
                    kc = half * 8 + j
                    dst = self.hT[:, kc, tt_ * 128:(tt_ + 1) * 128]
                    self.ts("dve", dst, pv[:, j * 128:(j + 1) * 128], self.sc1[:, kc:kc + 1],
                            self.modc[:, kc:kc + 1], ALU.mult, ALU.add, r=[psk, "sc1", "modc"], w=[("hT", tt_, kc)])
        if f"hT{l}" in self.dbg:
            o = self.dout(f"dbg_hT{l}", [128, KC, T], BF16)
            self.dma("sp", o, self.hT[:], r=[("hT", t, k) for t in range(NT) for k in range(KC)], final=True)

    def load_w(self, l, c0, ncols, prefetch=True):
        return self.next_w("in", l, c0, ncols, prefetch)

    def proj_feat(self, wbt, wk, fc0, M, tb, pst, psk):
        for kc in range(KC):
            self.mm(pst[0:M, :], wbt[:, kc, fc0:fc0 + M], self.hT[:, kc, tb * 512:(tb + 1) * 512],
                    start=(kc == 0), stop=(kc == KC - 1), r=[wk] + [("hT", tb * 4 + i, kc) for i in range(4)], w=[psk])

    def proj_tok(self, wbt, wk, c0, N, tt_, pst, psk, o0=0):
        for kc in range(KC):
            self.mm(pst[:, o0:o0 + N], self.hT[:, kc, tt_ * 128:(tt_ + 1) * 128], wbt[:, kc, c0:c0 + N],
                    start=(kc == 0), stop=(kc == KC - 1), r=[wk, ("hT", tt_, kc)], w=[psk])

    def group0(self, l):
        wbt, wk = self.load_w(l, C_B0, 144)
        for tb in range(4):
            sl = slice(tb * 512, (tb + 1) * 512)
            pst, psk = self.ps[tb % 2], f"ps{tb % 2}"
            self.proj_feat(wbt, wk, 0, 16, tb, pst, psk)
            self.cp("act", self.lora[0:16, sl], pst[0:16, :], r=[psk], w=["fC"])
            pst, psk = self.ps[2 + tb % 2], f"ps{2 + tb % 2}"
            self.proj_feat(wbt, wk, 16, 128, tb, pst, psk)
            self.act(self.wlal[0:64, sl], pst[0:64, :], AF.Tanh, r=[psk], w=["wlal"])
            self.cp("act", self.wlal[64:128, sl], pst[64:128, :], r=[psk], w=["wlal"])
        self.dump(f"lora{l}", self.lora, [16, T], "fC", BF16)
        self.dump(f"wlal{l}", self.wlal[:], [128, T], "wlal", BF16)

    def cm(self, l, g):
        wA, wAk = self.load_w(l, C_CM + g * 384, 256)
        wB, wBk = self.load_w(l, C_CM + g * 384 + 256, 128, prefetch=False)
        for tt_ in range(NT):
            pst, psk = self.ps[tt_ % 2], f"ps{tt_ % 2}"
            self.proj_tok(wA, wAk, 0, 256, tt_, pst, psk)
            self.proj_tok(wB, wBk, 0, 128, tt_, pst, psk, o0=256)
            u, v, gt = pst[:, 0:128], pst[:, 128:256], pst[:, 256:384]
            ss = self.small[:, 2:3]
            self.act(self.junk[:, 0:128], v, AF.Square, r=[psk], w=["junk", "small2"], accum_out=ss)
            rs = self.small[:, 3:4]
            self.rstd_from_ss(ss, rs, 128, EPS, "small2", "small3")
            vn = self.ot[:, 0:128]
            self.stt(vn, v, rs, self.pbt[:, 768 + g * 128:768 + (g + 1) * 128], ALU.mult, ALU.mult,
                     r=[psk, "small3", "pbt"], w=["ot"])
            sgt = self.junk[:, 128:256]
            self.act(sgt, gt, AF.Silu, r=[psk], w=["junk"])
            ug = self.otf[:, 0:128]
            self.tt("dve", ug, u, sgt, ALU.mult, r=[psk, "junk"], w=["otf"])
            ps2, ps2k = self.ps[2 + tt_ % 2], f"ps{2 + tt_ % 2}"
            self.mm(ps2[:, 0:128], self.wsT[:, g, :], vn, r=["wsT", "ot"], w=[ps2k])
            oo = self.skt[tt_ % 2][:, 0:128]
            ok = f"skt{tt_ % 2}"
            self.stt(oo, ps2[:, 0:128], self.bsT[:, g:g + 1], ug, ALU.add, ALU.mult, r=[ps2k, "bsT", "otf"], w=[ok])
            col = 768 + g * 128
            self.dma("sp", self.o_hbm[tt_ * 128:(tt_ + 1) * 128, col:col + 128], oo, r=[ok], w=[("o", tt_)])

    def seg_boundary(self, kind, l, d, p, tile_idx, pos, ncol, st_out):
        Hf, Hb = self.Hf, self.Hb
        fwd = (d == 0)
        W = 2 * ncol
        if pos == 0:
            s0 = (self.s0_gla if kind == "gla" else self.s0_rw)[l, d, p]
            self.memset("dve", Hf[:, 0:W], 0.0, w=["Hf"])
            for j in range(2):
                self.dma("sp", Hf[j * 64:(j + 1) * 64, j * ncol:(j + 1) * ncol], s0[j * 64:(j + 1) * 64, :], w=["Hf"])
            self.cp("dve", Hb[:, 0:W], Hf[:, 0:W], r=["Hf"], w=["Hb"])
        elif pos % 2 == 0:
            prev_tile = tile_idx - 1 if fwd else tile_idx + 1
            self.seg_out(kind, l, d, p, prev_tile // 2, ncol, st_out)
            self.ts("dve", Hf[:, 0:W], Hf[:, 0:W], self.carry_t[:, 0:1], None, ALU.mult,
                    r=["Hf", "carry"], w=["Hf"])
            self.cp("dve", Hb[:, 0:W], Hf[:, 0:W], r=["Hf"], w=["Hb"])

    def seg_out(self, kind, l, d, p, seg, ncol, st_out):
        for j in range(2):
            self.dma("sp", st_out[seg, l, d, p, j * 64:(j + 1) * 64, :],
                     self.Hf[j * 64:(j + 1) * 64, j * ncol:(j + 1) * ncol], r=["Hf"], final=True)

    def state_update(self, psh, pshk, W, bdt, bdk, tt_, pos):
        wcb, wcbk = self.wcbd[0], "wcbd0"
        self.ts("pool", wcb[:, 0:W], bdt[:, 0:W], self.wc[:, tt_:tt_ + 1], None, ALU.mult, r=[bdk, "wc"], w=[wcbk])
        self.tt("dve", self.t2[:, 0:W], psh[:, 0:W], self.Hf[:, 0:W], ALU.add, r=[pshk, "Hf"], w=["t2"])
        self.tt("pool", self.Hf[:, 0:W], self.t2[:, 0:W], wcb[:, 0:W], ALU.mult, r=["t2", wcbk], w=["Hf"])
        self.tt("dve", self.Hb[:, 0:W], self.t2[:, 0:W], wcb[:, 0:W], ALU.mult, r=["t2", wcbk], w=["Hb"])

    def scan_cum(self, la, lak, cum, cumk, tmp, tmpk, fwd):
        self.S.op("dve", lambda en: en.tensor_tensor_scan(out=cum[:], data0=self.scanm[:], data1=la[:],
                                                          initial=0.0, op0=ALU.mult, op1=ALU.add),
                  reads=[lak, "scanm"], writes=[cumk])
        if not fwd:
            c3 = cum[:].rearrange("p (c t) -> p c t", t=128)
            tot = c3[:, :, 127:128].to_broadcast([128, 4, 128])
            e3 = tmp[:].rearrange("p (c t) -> p c t", t=128)
            self.tt("dve", e3, tot, c3, ALU.subtract, r=[cumk], w=[tmpk])
            self.tt("dve", cum[:], tmp[:], la[:], ALU.add, r=[tmpk, lak], w=[cumk])

    def gla(self, l, p):
        wA, wAk = self.load_w(l, C_GLA + p * 768, 256)
        raw = [self.dAR[:, 0, :], self.dAR[:, 1, :]]
        for tb in range(4):
            sl = slice(tb * 512, (tb + 1) * 512)
            cst, snt = self.csb[0][0], self.csb[1][0]
            csk, snk = "cs0_0", "cs1_0"
            self.dma("pool", cst[:], self.cosT[:, sl], w=[csk])
            self.dma("pool", snt[:], self.sinS[:, sl], w=[snk])
            for which in range(2):
                pst, psk = self.ps[which], f"ps{which}"
                self.proj_feat(wA, wAk, which * 128, 128, tb, pst, psk)
                self.cp("act", raw[which][:, sl], pst[:, :], r=[psk], w=["dAR"])
                ps2, ps2k = self.ps[2 + which], f"ps{2 + which}"
                self.mm(ps2[:, :], self.pm[:], raw[which][:, sl], r=["pm", "dAR"], w=[ps2k])
                t1, t1k = self.f1[0], "f1_0"
                t2, t2k = self.f1[1], "f1_1"
                self.tt("dve", t2[:], ps2[:, :], snt[:], ALU.mult, r=[ps2k, snk], w=[t2k])
                self.tt("pool", t1[:], raw[which][:, sl], cst[:], ALU.mult, r=["dAR", csk], w=[t1k])
                dst = (self.fA if which == 0 else self.fB)
                self.tt("pool", dst[:, sl], t1[:], t2[:], ALU.add, r=[t1k, t2k], w=["fA" if which == 0 else "fB"])
        wV, wVk = self.load_w(l, C_GLA + p * 768 + 256, 256)
        for tt_ in range(NT):
            pst, psk = self.ps[4 + tt_ % 2], f"ps{4 + tt_ % 2}"
            self.proj_tok(wV, wVk, 0, 256, tt_, pst, psk)
            self.cp("dve", self.vtok[:, tt_, :], pst[:, 0:256], r=[psk], w=[("vtok", tt_)])
        wG, wGk = self.load_w(l, C_GLA + p * 768 + 512, 256)
        for tt_ in range(NT):
            pst, psk = self.ps[6 + tt_ % 2], f"ps{6 + tt_ % 2}"
            self.proj_tok(wG, wGk, 0, 256, tt_, pst, psk)
            self.act(self.sg[:, tt_, :], pst[:, 0:256], AF.Silu, r=[psk], w=[("sg", tt_)])
        self.dump(f"gla_q{l}_{p}", self.fA[:], [128, T], "fA", BF16)
        self.dump(f"gla_k{l}_{p}", self.fB[:], [128, T], "fB", BF16)
        for d in range(2):
            fwd = (d == 0)
            for tb in range(4):
                sl = slice(tb * 512, (tb + 1) * 512)
                pst, psk = self.ps[tb % 2], f"ps{tb % 2}"
                self.mm(pst[:, :], self.a2t[:, d, p * 128:(p + 1) * 128], self.lora[:, sl], r=["a2t", "fC"], w=[psk])
                e, ek = self.f1[0], "f1_0"
                la, lak = self.f1[1], "f1_1"
                cum, cumk = self.f1[2], "f1_2"
                self.act(e[:], pst[:, :], AF.Exp, r=[psk, "gbt"], w=[ek], scale=-1.0, bias=self.gbt[:, d, p:p + 1])
                self.act(e[:], e[:], AF.Ln, r=[ek], w=[ek], bias=1.0)
                self.ts("dve", la[:], e[:], -1.0 / 16.0, -1.0, ALU.mult, ALU.max, r=[ek], w=[lak])
                self.scan_cum(la, lak, cum, cumk, e, ek, fwd)
                c3 = cum[:].rearrange("p (c t) -> p c t", t=128)
                endcol = c3[:, :, 127] if fwd else c3[:, :, 0]
                self.act(self.wc[:, tb * 4:(tb + 1) * 4], endcol, AF.Exp, r=[cumk], w=["wc"])
                e1, e1k = self.f1[0], "f1_0"
                e2, e2k = self.f1[3], "f1_3"
                self.act(e1[:], cum[:], AF.Exp, r=[cumk, "small8"], w=[e1k], bias=self.small[:, 8:9])
                self.act(e2[:], cum[:], AF.Exp, r=[cumk], w=[e2k], scale=-1.0)
                self.tt("dve", self.dBK[:, 0, sl], self.fA[:, sl], e1[:], ALU.mult, r=["fA", e1k], w=["dBK"])
                self.tt("pool", self.dBK[:, 1, sl], self.fB[:, sl], e2[:], ALU.mult, r=["fB", e2k], w=["dBK"])
                pst2, ps2k = self.ps[2 + tb % 2], f"ps{2 + tb % 2}"
                pv = pst2[:, :].bitcast(BF16)
                for i in range(4):
                    tt_ = tb * 4 + i
                    self.tr(pv[:, i * 128:(i + 1) * 128], self.dBK[:, 1, tt_ * 128:(tt_ + 1) * 128], self.ident[:],
                            r=["dBK", "ident"], w=[ps2k])
                for i in range(4):
                    tt_ = tb * 4 + i
                    self.cp("dve", self.dtok[:, tt_, 0:128], pv[:, i * 128:(i + 1) * 128], r=[ps2k], w=[("dtok", tt_)])
            self.dump(f"gla_qt{l}_{p}_{d}", self.dBK[:, 0, :], [128, T], "dBK", BF16)
            self.dump(f"gla_kt{l}_{p}_{d}", self.dBK[:, 1, :], [128, T], "dBK", BF16)
            order = list(range(NT)) if fwd else list(range(NT - 1, -1, -1))
            for pos, tt_ in enumerate(order):
                self.seg_boundary("gla", l, d, p, tt_, pos, 128, self.st_gla)
                tsl = slice(tt_ * 128, (tt_ + 1) * 128)
                def gla_scores(pos_, t_):
                    sk_, skk_ = self.skt[pos_ % 2], f"skt{pos_ % 2}"
                    ts_ = slice(t_ * 128, (t_ + 1) * 128)
                    for j in range(2):
                        rows = slice(j * 64, (j + 1) * 64)
                        pss, pssk = self.ps[j], f"ps{j}"
                        self.mm(pss[:, 0:128], self.dBK[rows, 1, ts_], self.dBK[rows, 0, ts_], r=["dBK"], w=[pssk])
                        self.tt("dve", sk_[:, j * 128:(j + 1) * 128], pss[:, 0:128], self.m4[d][:, 128:256], ALU.mult,
                                r=[pssk, f"m4_{d}"], w=[skk_])
                if pos == 0:
                    gla_scores(0, tt_)
                if pos + 1 < NT:
                    gla_scores(pos + 1, order[pos + 1])
                skt, sktk = self.skt[pos % 2], f"skt{pos % 2}"
                psy, psyk = self.ps[5 + pos % 2], f"ps{5 + pos % 2}"
                self.mm(psy[:, 0:256], self.dBK[:, 0, tsl], self.Hb[:, 0:256], start=True, stop=False,
                        r=["dBK", "Hb"], w=[psyk])
                for j in range(2):
                    self.mm(psy[:, j * 128:(j + 1) * 128], skt[:, j * 128:(j + 1) * 128],
                            self.vtok[:, tt_, j * 128:(j + 1) * 128], start=False, stop=(j == 1),
                            r=[sktk, ("vtok", tt_)], w=[psyk])
                if d == 0:
                    self.cp("act", self.yacc[:, tt_, :], psy[:, 0:256], r=[psyk], w=[("yacc", tt_)])
                else:
                    self.tt("dve", self.yacc[:, tt_, :], psy[:, 0:256], self.yacc[:, tt_, :], ALU.add,
                            r=[psyk, ("yacc", tt_)], w=[("yacc", tt_)])
                psh, pshk = self.ps[7], "ps7"
                self.mm(psh[:, 0:256], self.dtok[:, tt_, 0:128], self.vtok[:, tt_, :], r=[("dtok", tt_), ("vtok", tt_)],
                        w=[pshk])
                self.state_update(psh, pshk, 256, self.bd, "bd", tt_, pos)
            self.seg_out("gla", l, d, p, order[-1] // 2, 128, self.st_gla)
        self.dump(f"gla_y{l}_{p}", self.yacc_t[:, :], [128, 4096], ("yacc", 0))
        for tt_ in range(NT):
            for j in range(2):
                self.act(self.junk[:, 0:128], self.yacc[:, tt_, j * 128:(j + 1) * 128], AF.Square,
                         r=[("yacc", tt_)], w=["junk", "small10"], accum_out=self.small[:, 10 + j:11 + j])
            self.rstd_from_ss(self.small[:, 10:12], self.small[:, 12:14], 128, EPS, "small10", "small12")
            for j in range(2):
                self.stt(self.otf[:, j * 128:(j + 1) * 128], self.yacc[:, tt_, j * 128:(j + 1) * 128],
                         self.small[:, 12 + j:13 + j], self.pbt[:, p * 256 + j * 128:p * 256 + (j + 1) * 128],
                         ALU.mult, ALU.mult, r=[("yacc", tt_), "small12", "pbt"], w=["otf"])
            self.tt("pool", self.ot[:], self.otf[:], self.sg[:, tt_, :], ALU.mult, r=["otf", ("sg", tt_)], w=["ot"])
            col = p * 256
            self.dma("sp", self.o_hbm[tt_ * 128:(tt_ + 1) * 128, col:col + 256], self.ot[:], r=["ot"], w=[("o", tt_)])

    def rwkv(self, l, p):
        import os
        CUT = float(os.environ.get("RW_CUT", "99"))
        C0 = math.exp(-0.5)
        wA, wAk = self.load_w(l, C_RW + p * 512, 256)
        dsts = [(self.fA, "fA"), (self.fB, "fB")]
        for which in range(2):
            for tb in range(4):
                sl = slice(tb * 512, (tb + 1) * 512)
                pst, psk = self.ps[tb % 2], f"ps{tb % 2}"
                self.proj_feat(wA, wAk, which * 128, 128, tb, pst, psk)
                self.cp("act", dsts[which][0][:, sl], pst[:, :], r=[psk], w=[dsts[which][1]])
        if CUT <= 0.2:
            return
        wB, wBk = self.load_w(l, C_RW + p * 512 + 256, 256)
        kkw = self.pct[:, p, 0:1]
        ka = self.pct[:, p, 1:2]
        rk = self.pct[:, p, 2:3]
        omka = self.omka[:, p:p + 1]
        for tb in range(4):
            sl = slice(tb * 512, (tb + 1) * 512)
            sq = self.dAR[:, 0, sl]
            self.act(sq, self.fB[:, sl], AF.Square, r=["fB", "pct"], w=["dAR"], scale=kkw)
            ps2, ps2k = self.ps[2 + tb % 2], f"ps{2 + tb % 2}"
            self.mm(ps2[:, :], self.bd64[:], sq, r=["bd64", "dAR"], w=[ps2k])
            rn, rnk = self.f1[0], "f1_0"
            self.ts("dve", rn[:], ps2[:, :], 1e-24, None, ALU.max, r=[ps2k], w=[rnk])
            self.act(rn[:], rn[:], AF.Sqrt, r=[rnk], w=[rnk])
            self.S.op("dve", lambda e, rn=rn: e.reciprocal(out=rn[:], in_=rn[:]), reads=[rnk], writes=[rnk])
            self.stt(self.fC[:, sl], self.fB[:, sl], kkw, rn[:], ALU.mult, ALU.mult, r=["fB", "pct", rnk], w=["fC"])
        if CUT <= 0.5:
            return
        for tt_ in range(NT):
            pst, psk = self.ps[4 + tt_ % 2], f"ps{4 + tt_ % 2}"
            self.proj_tok(wB, wBk, 0, 128, tt_, pst, psk)
            self.cp("dve", self.vtok[:, tt_, 0:128], pst[:, 0:128], r=[psk], w=[("vtok", tt_)])
        for tt_ in range(NT):
            pst, psk = self.ps[6 + tt_ % 2], f"ps{6 + tt_ % 2}"
            self.proj_tok(wB, wBk, 128, 128, tt_, pst, psk)
            self.act(self.sg[:, tt_, 0:128], pst[:, 0:128], AF.Silu, r=[psk], w=[("sg", tt_)])
        if CUT <= 1:
            return
        self.dump(f"rw_r{l}_{p}", self.fA[:], [128, T], "fA", BF16)
        self.dump(f"rw_kk{l}_{p}", self.fC[:], [128, T], "fC", BF16)
        for d in range(2):
            fwd = (d == 0)
            for tb in range(4):
                sl = slice(tb * 512, (tb + 1) * 512)
                pst, psk = self.ps[tb % 2], f"ps{tb % 2}"
                self.mm(pst[:, :], self.w2ra[0:64, d, p * 128:(p + 1) * 128], self.wlal[0:64, sl], r=["w2ra", "wlal"], w=[psk])
                la, lak = self.f1[1], "f1_1"
                cum, cumk = self.f1[2], "f1_2"
                tmp, tmpk = self.f1[0], "f1_0"
                ex, exk = self.f1[3], "f1_3"
                self.act(la[:], pst[:, :], AF.Sigmoid, r=[psk, "bct"], w=[lak], bias=self.bct[:, d, p, 0:1])
                self.scan_cum(la, lak, cum, cumk, tmp, tmpk, fwd)
                c3 = cum[:].rearrange("p (c t) -> p c t", t=128)
                endcol = c3[:, :, 127] if fwd else c3[:, :, 0]
                self.act(self.wc[:, tb * 4:(tb + 1) * 4], endcol, AF.Exp, r=[cumk], w=["wc"], scale=-C0)
                self.act(ex[:], cum[:], AF.Exp, r=[cumk], w=[exk], scale=-C0)
                self.tt("dve", self.dAR[:, 1, sl], self.fA[:, sl], ex[:], ALU.mult, r=["fA", exk], w=["dAR"])
                self.tt("pool", tmp[:], cum[:], la[:], ALU.subtract, r=[cumk, lak], w=[tmpk])
                self.act(tmp[:], tmp[:], AF.Exp, r=[tmpk], w=[tmpk], scale=-C0)
                self.stt(self.dAR[:, 0, sl], self.fC[:, sl], -1.0, tmp[:], ALU.mult, ALU.mult, r=["fC", tmpk], w=["dAR"])
                self.act(ex[:], cum[:], AF.Exp, r=[cumk], w=[exk], scale=C0)
                pst2, ps2k = self.ps[2 + tb % 2], f"ps{2 + tb % 2}"
                self.mm(pst2[:, :], self.w2ra[64:128, d, p * 128:(p + 1) * 128], self.wlal[64:128, sl], r=["w2ra", "wlal"], w=[ps2k])
                ic, ick = self.f1[1], "f1_1"
                self.act(ic[:], pst2[:, :], AF.Sigmoid, r=[ps2k, "bct"], w=[ick], bias=self.bct[:, d, p, 1:2])
                self.tt("pool", tmp[:], ic[:], ex[:], ALU.mult, r=[ick, exk], w=[tmpk])
                self.tt("dve", self.dBK[:, 0, sl], self.fC[:, sl], tmp[:], ALU.mult, r=["fC", tmpk], w=["dBK"])
                self.ts("dve", ic[:], ic[:], ka, omka, ALU.mult, ALU.add, r=[ick, "pct", "omka"], w=[ick])
                self.tt("pool", ic[:], ic[:], self.fB[:, sl], ALU.mult, r=[ick, "fB"], w=[ick])
                self.tt("dve", self.dBK[:, 1, sl], ic[:], ex[:], ALU.mult, r=[ick, exk], w=["dBK"])
                bprod = self.f1[0][:, :].bitcast(BF16)[:, 0:512]
                self.stt(bprod, ic[:], rk, self.fA[:, sl], ALU.mult, ALU.mult, r=[ick, "pct", "fA"], w=[tmpk])
                psb, psbk = self.ps[4 + tb % 2], f"ps{4 + tb % 2}"
                for i in range(4):
                    self.mm(psb[:, 2 * i:2 * i + 2], bprod[:, i * 128:(i + 1) * 128], self.bind[:], r=[tmpk, "bind"], w=[psbk])
                self.cp("dve", self.bon[:, tb * 4:(tb + 1) * 4, d, :],
                        psb[:, 0:8].rearrange("p (a b) -> p a b", b=2), r=[psbk], w=["bon"])
                pst3, ps3k = self.ps[6 + tb % 2], f"ps{6 + tb % 2}"
                pv = pst3[:, :].bitcast(BF16)
                for i in range(4):
                    tt_ = tb * 4 + i
                    for which in range(2):
                        self.tr(pv[:, i * 256 + which * 128:i * 256 + (which + 1) * 128],
                                self.dBK[:, which, tt_ * 128:(tt_ + 1) * 128], self.ident[:], r=["dBK", "ident"], w=[ps3k])
                for i in range(4):
                    tt_ = tb * 4 + i
                    self.cp("dve", self.dtok[:, tt_, :], pv[:, i * 256:(i + 1) * 256], r=[ps3k], w=[("dtok", tt_)])
            if CUT <= 2:
                return
            self.dump(f"rw_ar{l}_{p}_{d}", self.dAR[:], [128, 2, T], "dAR", BF16)
            self.dump(f"rw_bk{l}_{p}_{d}", self.dBK[:], [128, 2, T], "dBK", BF16)
            order = list(range(NT)) if fwd else list(range(NT - 1, -1, -1))
            si_steps = self.rw_SI(d, order[0], 0)
            for st_ in si_steps:
                st_()
            for pos, tt_ in enumerate(order):
                par = pos % 2
                c_steps = self.rw_C(l, d, p, tt_, pos, par)
                n_steps = self.rw_SI(d, order[pos + 1], 1 - par) if pos + 1 < NT else []
                for i in range(max(len(c_steps), len(n_steps))):
                    if i < len(n_steps):
                        n_steps[i]()
                    if i < len(c_steps):
                        c_steps[i]()
            self.seg_out("rw", l, d, p, order[-1] // 2, 64, self.st_rw)
            if CUT <= 5:
                return
        if CUT <= 6:
            return
        self.dump(f"rw_y{l}_{p}", self.yacc_t[:, :], [128, 4096], ("yacc", 0))
        for tt_ in range(NT):
            yv = self.yacc[:, tt_, 0:128]
            st = self.small[:, 16:28].rearrange("p (a b) -> p a b", b=6)
            for j in range(2):
                self.S.op("dve", lambda e, j=j, yv=yv, st=st: e.bn_stats(out=st[:, j, :], in_=yv[:, j * 64:(j + 1) * 64]),
                          reads=[("yacc", tt_)], writes=["small16"])
            mv = self.small[:, 28:32].rearrange("p (a b) -> p a b", b=2)
            for j in range(2):
                self.S.op("dve", lambda e, j=j, st=st, mv=mv: e.bn_aggr(out=mv[:, j, :], in_=st[:, j, :]),
                          reads=["small16"], writes=["small28"])
            rs = self.small[:, 32:34]
            self.ts("dve", rs, mv[:, :, 1], GN_EPS, None, ALU.add, r=["small28"], w=["small32"])
            self.act(rs, rs, AF.Sqrt, r=["small32"], w=["small32"])
            self.S.op("dve", lambda e, rs=rs: e.reciprocal(out=rs, in_=rs), reads=["small32"], writes=["small32"])
            bsum = self.small[:, 34:36]
            self.tt("dve", bsum, self.bon[:, tt_, 0, :], self.bon[:, tt_, 1, :], ALU.add, r=["bon"], w=["small34"])
            for j in range(2):
                cs = slice(j * 64, (j + 1) * 64)
                self.ts("dve", self.otf[:, cs], yv[:, cs], mv[:, j, 0:1], self.small[:, 32 + j:33 + j], ALU.subtract, ALU.mult,
                        r=[("yacc", tt_), "small28", "small32"], w=["otf"])
            self.tt("pool", self.otf[:, 0:128], self.otf[:, 0:128], self.pbt[:, 1280 + p * 128:1280 + (p + 1) * 128], ALU.mult,
                    r=["otf", "pbt"], w=["otf"])
            for j in range(2):
                cs = slice(j * 64, (j + 1) * 64)
                self.stt(self.otf[:, cs], self.vtok[:, tt_, cs], self.small[:, 34 + j:35 + j], self.otf[:, cs], ALU.mult, ALU.add,
                         r=[("vtok", tt_), "small34", "otf"], w=["otf"])
            self.tt("pool", self.ot[:, 0:128], self.otf[:, 0:128], self.sg[:, tt_, 0:128], ALU.mult, r=["otf", ("sg", tt_)], w=["ot"])
            col = 1280 + p * 128
            self.dma("sp", self.o_hbm[tt_ * 128:(tt_ + 1) * 128, col:col + 128], self.ot[:, 0:128], r=["ot"], w=[("o", tt_)])

    def rw_SI(self, d, tt_, par):
        tsl = slice(tt_ * 128, (tt_ + 1) * 128)
        sk = self.skt if par == 0 else self.sktB
        skk = [f"skt{j}" if par == 0 else f"sktB{j}" for j in range(2)]
        bank = [self.ps[3], self.ps[4]]
        bkk = ["ps3", "ps4"]
        steps = []

        def s_scores():
            for j in range(2):
                rows = slice(j * 64, (j + 1) * 64)
                pss, pssk = self.ps[j], f"ps{j}"
                self.mm(pss[:, 0:256], self.dBK[rows, 0, tsl], self.dAR[rows, :, tsl], r=["dBK", "dAR"], w=[pssk])
                self.mm(pss[:, 256:512], self.dBK[rows, 1, tsl], self.dAR[rows, :, tsl], r=["dBK", "dAR"], w=[pssk])
                self.tt("dve", sk[j][:, :], pss[:, :], self.m4[d][:, :], ALU.mult, r=[pssk, f"m4_{d}"], w=[skk[j]])
            pv = self.ps[2][:, :].bitcast(BF16)
            for j in range(2):
                self.tr(pv[:, j * 128:(j + 1) * 128], sk[j][:, 0:128], self.ident[:], r=[skk[j], "ident"], w=["ps2"])
            for j in range(2):
                self.cp("dve", self.pq[j][0][:, 128:256], pv[:, j * 128:(j + 1) * 128], r=["ps2"], w=[("pq", j, 0)])
        steps.append(s_scores)

        def s_lvl0():
            for j in range(2):
                Q0 = sk[j][:, 0:128]
                P0 = self.pq[j][0][:, 128:256]
                self.mm(bank[j][:, 0:128], P0, Q0, r=[("pq", j, 0), skk[j]], w=[bkk[j]])
                self.mm(bank[j][:, 128:256], Q0, P0, r=[("pq", j, 0), skk[j]], w=[bkk[j]])
            for j in range(2):
                Q0 = sk[j][:, 0:128]
                self.cp("act", self.pq[j][1][:, 0:256], bank[j][:, 0:256], r=[bkk[j]], w=[("pq", j, 1)])
                self.tt("pool", self.rr[j][1][:], Q0, self.ident[:], ALU.add, r=[skk[j], "ident"], w=[("rr", j, 1)])
        steps.append(s_lvl0)

        def mk_lvl(k):
            def s_lvl():
                b = k % 2
                for j in range(2):
                    Qk = self.pq[j][b][:, 0:128]
                    Pk = self.pq[j][b][:, 128:256]
                    Rk = self.rr[j][b][:]
                    if k < 6:
                        self.mm(bank[j][:, 0:128], Pk, Qk, r=[("pq", j, b)], w=[bkk[j]])
                        self.mm(bank[j][:, 128:256], Qk, Pk, r=[("pq", j, b)], w=[bkk[j]])
                    self.mm(self.ps[j][:, 0:128], Pk, Rk, r=[("pq", j, b), ("rr", j, b)], w=[f"ps{j}"])
                for j in range(2):
                    Rk = self.rr[j][b][:]
                    if k < 6:
                        self.cp("act", self.pq[j][1 - b][:, 0:256], bank[j][:, 0:256], r=[bkk[j]], w=[("pq", j, 1 - b)])
                        self.tt("dve", self.rr[j][1 - b][:], self.ps[j][:, 0:128], Rk, ALU.add,
                                r=[f"ps{j}", ("rr", j, b)], w=[("rr", j, 1 - b)])
                    else:
                        self.tt("dve", self.rfin[j][par][:], self.ps[j][:, 0:128], Rk, ALU.add,
                                r=[f"ps{j}", ("rr", j, b)], w=[("rfin", j, par)])
            return s_lvl
        for k in range(1, 7):
            steps.append(mk_lvl(k))
        return steps

    def rw_C(self, l, d, p, tt_, pos, par):
        tsl = slice(tt_ * 128, (tt_ + 1) * 128)
        sk = self.skt if par == 0 else self.sktB
        skk = [f"skt{j}" if par == 0 else f"sktB{j}" for j in range(2)]
        psx, psxk = self.ps[5], "ps5"
        psy, psyk = self.ps[6], "ps6"
        psh, pshk = self.ps[7], "ps7"

        def c_x():
            self.seg_boundary("rw", l, d, p, tt_, pos, 64, self.st_rw)
            self.mm(psx[:, 0:128], self.dAR[:, 0, tsl], self.Hb[:, 0:128], start=True, stop=False, r=["dAR", "Hb"], w=[psxk])
            for j in range(2):
                self.mm(psx[:, j * 64:(j + 1) * 64], sk[j][:, 256:384], self.vtok[:, tt_, j * 64:(j + 1) * 64],
                        start=False, stop=(j == 1), r=[skk[j], ("vtok", tt_)], w=[psxk])
            self.cp("act", self.xu[:, 0, :], psx[:, 0:128], r=[psxk], w=["xu0"])

        def c_u():
            for j in range(2):
                self.mm(psx[:, 128 + j * 64:128 + (j + 1) * 64], self.rfin[j][par][:], self.xu[:, 0, j * 64:(j + 1) * 64],
                        r=[("rfin", j, par), "xu0"], w=[psxk])
            self.cp("act", self.xu[:, 1, :], psx[:, 128:256], r=[psxk], w=["xu1"])

        def c_h():
            self.mm(psh[:, 0:128], self.dtok[:, tt_, 0:128], self.xu[:, 1, :], start=True, stop=False,
                    r=[("dtok", tt_), "xu1"], w=[pshk])
            self.mm(psh[:, 0:128], self.dtok[:, tt_, 128:256], self.vtok[:, tt_, 0:128], start=False, stop=True,
                    r=[("dtok", tt_), ("vtok", tt_)], w=[pshk])

        def c_y():
            self.mm(psy[:, 0:128], self.dAR[:, 1, tsl], self.Hb[:, 0:128], start=True, stop=False, r=["dAR", "Hb"], w=[psyk])
            for j in range(2):
                cs = slice(j * 64, (j + 1) * 64)
                self.mm(psy[:, cs], sk[j][:, 128:256], self.xu[:, 1, cs], start=False, stop=False,
                        r=[skk[j], "xu1"], w=[psyk])
                self.mm(psy[:, cs], sk[j][:, 384:512], self.vtok[:, tt_, cs], start=False, stop=(j == 1),
                        r=[skk[j], ("vtok", tt_)], w=[psyk])
            self.state_update(psh, pshk, 128, self.bd64f, "bd64f", tt_, pos)
            if d == 0:
                self.cp("act", self.yacc[:, tt_, 0:128], psy[:, 0:128], r=[psyk], w=[("yacc", tt_)])
            else:
                self.tt("dve", self.yacc[:, tt_, 0:128], psy[:, 0:128], self.yacc[:, tt_, 0:128], ALU.add,
                        r=[psyk, ("yacc", tt_)], w=[("yacc", tt_)])
        return [c_x, c_u, c_h, c_y]

    def rwkv_all(self, l):
        for p in range(6):
            self.rwkv(l, p)

    def phase3(self, l, last):
        wv = self.w_out[l].rearrange("(kc p) n -> p kc n", p=128)
        for nb in range(4):
            self.dma("pool", self.hT[:, :, nb * 512:(nb + 1) * 512], wv[:, :, nb * 512:(nb + 1) * 512],
                     w=[("hT", nb * 4 + i, k) for i in range(4) for k in range(KC)])
        if last:
            self.dma("sp", self.pbt[:], self.fnw_b[:, :], w=["pbt"])
        dst = self.y if last else self.x1
        src = self.x if l == 0 else self.x1
        obf = self.vtok[:, 0:8, :].rearrange("p a b -> p (a b)")
        obk = [("vtok", i) for i in range(8)]
        oT = self.vtok[:, 8:16, :].rearrange("p a b -> p (a b)")
        oTk = [("vtok", 8 + i) for i in range(8)]
        for tt_ in range(NT):
            rows = slice(tt_ * 128, (tt_ + 1) * 128)
            self.dma("sp", obf, self.o_hbm[rows, :], r=[("o", tt_)], w=obk)
            for half in range(2):
                pst, psk = self.ps[half], f"ps{half}"
                pv = pst[:, :].bitcast(BF16)
                for j in range(8):
                    kc = half * 8 + j
                    self.tr(pv[:, j * 128:(j + 1) * 128], obf[:, kc * 128:(kc + 1) * 128], self.ident[:],
                            r=obk + ["ident"], w=[psk])
                self.cp("dve", oT[:, half * 1024:(half + 1) * 1024], pv[:, :], r=[psk], w=oTk[half * 4:half * 4 + 4])
            xt, xk = self.xt[tt_ % 2], self.xtk(tt_ % 2)
            self.dma("sp", xt, src[rows, :], r=([("x1", tt_)] if l else []), w=xk)
            for nb in range(4):
                pst, psk = self.ps[2 + nb], f"ps{2 + nb}"
                for kc in range(KC):
                    self.mm(pst[:, :], oT[:, kc * 128:(kc + 1) * 128], self.hT[:, kc, nb * 512:(nb + 1) * 512],
                            start=(kc == 0), stop=(kc == KC - 1),
                            r=oTk + [("hT", nb * 4 + i, kc) for i in range(4)], w=[psk])
                sl = slice(nb * 512, (nb + 1) * 512)
                t1, t1k = self.f1[nb % 2], f"f1_{nb % 2}"
                self.tt("dve", t1[:], pst[:, :], self.gate_b[:, sl], ALU.mult, r=[psk, "gate_b"], w=[t1k])
                self.tt("pool", xt[:, sl], xt[:, sl], t1[:], ALU.add, r=xk + [t1k], w=xk)
            if last:
                ss = self.small[:, 4:5]
                jk = self.dAR[:, :, :].rearrange("p a b -> p (a b)")[:, 0:2048]
                self.act(jk, xt, AF.Square, r=xk, w=["dAR", "small4"], accum_out=ss)
                rs = self.small[:, 5:6]
                self.rstd_from_ss(ss, rs, D, EPS, "small4", "small5")
                self.stt(xt, xt, rs, self.pbt[:], ALU.mult, ALU.mult, r=xk + ["small5", "pbt"], w=xk)
                self.dma("sp", dst[rows, :], xt, r=xk, final=True)
            else:
                self.dma("sp", dst[rows, :], xt, r=xk, w=[("x1", tt_)])

    def build(self):
        self.declare()
        self.alloc()
        self.load_consts()
        for l in range(DEPTH):
            self.load_layer_params(l)
            self.modulation(l)
            if self.stop == f"mod_{l}":
                break
            self.phase1(l)
            if self.stop == f"p1_{l}":
                break
            self.group0(l)
            if self.stop == f"g0_{l}":
                break
            if self.stop == f"p3only_{l}":
                self.phase3(l, True)
                break
            if self.stop == f"rwonly_{l}":
                self.rwkv(l, 0)
                break
            if self.stop == f"cmonly_{l}":
                self.cm(l, 1)
                break
            ng = 1 if self.stop == f"gla1_{l}" else 3
            for p in range(ng):
                self.gla(l, p)
            if self.stop in (f"gla_{l}", f"gla1_{l}"):
                break
            for g in range(4):
                self.cm(l, g)
            if self.stop == f"cm_{l}":
                break
            nr = 1 if self.stop == f"rw1_{l}" else 6
            for p in range(nr):
                self.rwkv(l, p)
            if self.stop in (f"rw_{l}", f"rw1_{l}"):
                break
            self.phase3(l, l == DEPTH - 1)
        self.S.wait_all_outputs("sp")
        stats = self.S.emit()
        self.es.close()
        return stats


def _rope_tables(latent):
    cosT = np.ones((128, T), np.float32)
    sinS = np.zeros((128, T), np.float32)
    if latent:
        t = np.arange(T)
        row, col = t // 64, t % 64
        inv = (10000.0 ** (-np.arange(16, dtype=np.float32) / 16)).astype(np.float32)
        for dd in range(128):
            d64 = dd % 64
            blk = d64 // 32
            i = d64 % 16
            second = (d64 % 32) >= 16
            pos = (row if blk == 0 else col).astype(np.float32)
            ang = pos * inv[i]
            cosT[dd] = np.cos(ang)
            sinS[dd] = np.sin(ang) if second else -np.sin(ang)
    return cosT, sinS


def _consts():
    c = {}
    c["c_ident"] = np.eye(128, dtype=np.float32)
    pm = np.zeros((128, 128), np.float32)
    for dout_ in range(128):
        d32 = dout_ % 32
        partner = dout_ + 16 if d32 < 16 else dout_ - 16
        pm[partner, dout_] = 1.0
    c["c_pm"] = pm
    s = np.arange(128)[:, None]
    t = np.arange(128)[None, :]
    m4 = np.zeros((2, 128, 512), np.float32)
    for d, (strict, incl) in enumerate([((s < t), (s <= t)), ((s > t), (s >= t))]):
        m4[d, :, 0:128] = strict
        m4[d, :, 128:256] = incl
        m4[d, :, 256:384] = strict
        m4[d, :, 384:512] = incl
    c["c_m4"] = m4
    sm = np.ones((128, 512), np.float32)
    sm[:, ::128] = 0.0
    c["c_scanm"] = sm
    bd = np.zeros((128, 256), np.float32)
    bd[0:64, 0:128] = 1.0
    bd[64:128, 128:256] = 1.0
    c["c_bd"] = bd
    bd64 = np.zeros((128, 128), np.float32)
    bd64[0:64, 0:64] = 1.0
    bd64[64:128, 64:128] = 1.0
    c["c_bd64"] = bd64
    bind = np.zeros((128, 2), np.float32)
    bind[0:64, 0] = 1.0
    bind[64:128, 1] = 1.0
    c["c_bind"] = bind
    return c


def prep_inputs(inp):
    f = lambda a: np.ascontiguousarray(np.asarray(a, dtype=np.float32))
    perm = _col_perm()
    shared = {}
    shared["w_mod"] = f(inp["w_mod"])
    shared["b_mod"] = f(inp["b_mod"]).reshape(DEPTH, 24, 1, 256)
    shared["nwT"] = f(f(inp["norm_w"]).reshape(DEPTH, 16, 128).transpose(0, 2, 1))
    shared["fnw_b"] = f(np.broadcast_to(f(inp["final_norm_w"])[None, :], (128, D)))
    shared["w_in_r"] = f(f(inp["w_in"])[:, :, perm])
    shared["w_out"] = f(inp["w_out"])
    shared["a2aug"] = f(inp["gla_a2"])
    shared["w2aug"] = f(inp["rw_w2"])
    shared["ra2aug"] = f(inp["rw_a2"])
    bc = np.stack([f(inp["rw_w0"]).reshape(DEPTH, 2, 6, 128), f(inp["rw_a0"]).reshape(DEPTH, 2, 6, 128)], axis=-1)
    shared["bcols"] = f(bc.transpose(0, 3, 1, 2, 4))
    shared["gbcols"] = f(f(inp["gla_a_bias"]).reshape(DEPTH, 2, 3, 128).transpose(0, 3, 1, 2))
    pc = np.stack([f(inp["rw_kk"]).reshape(DEPTH, 6, 128), f(inp["rw_ka"]).reshape(DEPTH, 6, 128),
                   f(inp["rw_rk"]).reshape(DEPTH, 6, 128)], axis=-1)
    shared["pcols"] = f(pc.transpose(0, 2, 1, 3))
    pbrow = np.concatenate([f(inp["gla_norm_w"]), f(inp["cm_norm_w"]), f(inp["rw_gn_w"])], axis=1)
    shared["pb"] = f(np.broadcast_to(pbrow[:, None, :], (DEPTH, 128, 2048)))
    shared["cm_wsT"] = f(f(inp["cm_ws"]).transpose(0, 1, 3, 2))
    shared["cm_bsT"] = f(f(inp["cm_bs"]).transpose(0, 2, 1))
    shared.update(_consts())
    cos_l, sin_l = _rope_tables(True)
    cos_c, sin_c = _rope_tables(False)
    xp = f(inp["x_prompt"])
    xs = f(inp["x_sample"])
    sg = f(inp["state_gla"])
    sr = f(inp["state_rwkv"])
    maps = []
    for c in range(8):
        m = dict(shared)
        if c < 4:
            m["x"] = f(xp[8 * c:8 * c + 8].reshape(T, D))
            m["condT"] = f(f(inp["c_ctx"]).reshape(16, 128).T)
            m["cosT"], m["sinS"] = cos_c, sin_c
            m["s0_gla"] = np.zeros((DEPTH, 2, 3, 128, 128), np.float32)
            m["s0_rw"] = np.zeros((DEPTH, 2, 6, 128, 64), np.float32)
            m["carry"] = np.zeros((128, 1), np.float32)
        else:
            b = c - 4
            m["x"] = f(xs[b])
            m["condT"] = f(f(inp["c"])[b].reshape(16, 128).T)
            m["cosT"], m["sinS"] = cos_l, sin_l
            m["s0_gla"] = f(sg[b].reshape(DEPTH, 2, 3, 128, 128))
            m["s0_rw"] = f(sr[b].transpose(0, 1, 2, 4, 3).reshape(DEPTH, 2, 6, 128, 64))
            m["carry"] = np.ones((128, 1), np.float32)
        maps.append(m)
    return maps


_CACHE = {}


def kernel(**inputs):
    maps = prep_inputs(inputs)
    if "nc" not in _CACHE:
        b = Build()
        b.build()
        _CACHE["nc"] = b.nc
    res = run_bass_kernel_spmd(_CACHE["nc"], maps, core_ids=list(range(8)))
    R = res.results
    y_prompt = np.stack([R[c]["y"] for c in range(4)]).reshape(32, 256, D).astype(np.float32)
    y_sample = np.stack([R[c]["y"] for c in range(4, 8)]).astype(np.float32)
    sgl = np.stack([R[c]["st_gla"] for c in range(4)]).reshape(32, DEPTH, 2, 6, 64, 128).astype(np.float32)
    srw = np.stack([R[c]["st_rw"] for c in range(4)]).reshape(32, DEPTH, 2, 12, 64, 64)
    srw = np.ascontiguousarray(srw.transpose(0, 1, 2, 3, 5, 4)).astype(np.float32)
    return (y_prompt, y_sample, sgl, srw)
```
